# Optimizing a Trainium2 kernel written in Bass

```python
import math
import jax, jax.numpy as jnp
from jax import lax
import numpy as np

D_MODEL = 1024
BATCH = 8
SEQ = 4096
DEPTH = 2
DEC_BATCH = 16
DEC_SEQ = 64
PAST_LEN = 1024

CHUNK = 64
N_MEM = 256
CONV_DIM = D_MODEL
CONV_WIDTH = 31
D_INNER = 2 * D_MODEL
SSD_HEAD_DIM = 64
SSD_HEADS = D_INNER // SSD_HEAD_DIM
SSD_GROUPS = 4
SSD_HPG = SSD_HEADS // SSD_GROUPS
SSD_STATE = 128
SSD_CONV_WIDTH = 4
SSD_XBC = D_INNER + 2 * SSD_GROUPS * SSD_STATE
XA_HEADS = 4
XA_HEAD_DIM = D_MODEL // XA_HEADS
XA_DIM = XA_HEADS * XA_HEAD_DIM
N_BRANCH = 3
IN_SIZES = (CONV_DIM, CONV_DIM, CONV_DIM, D_INNER, SSD_XBC, SSD_HEADS, XA_DIM, XA_DIM, N_BRANCH * D_MODEL)
N_IN = sum(IN_SIZES)
EPS = 1e-6

kernel_name = 'hybrid_conformer_ssd_memxattn_stream_step'


def _rmsnorm(x, w):
    x32 = x.astype(jnp.float32)
    y = x32 * lax.rsqrt(jnp.mean(x32 * x32, axis=-1, keepdims=True) + EPS)
    return (y * w.astype(jnp.float32)).astype(x.dtype)


def _layernorm(x, w, b):
    x32 = x.astype(jnp.float32)
    mu = jnp.mean(x32, axis=-1, keepdims=True)
    xc = x32 - mu
    y = xc * lax.rsqrt(jnp.mean(xc * xc, axis=-1, keepdims=True) + EPS)
    return (y * w.astype(jnp.float32) + b.astype(jnp.float32)).astype(x.dtype)


def _causal_dwconv(x, state, w, b):
    xp = jnp.concatenate([state.astype(x.dtype), x], axis=1)
    y = lax.conv_general_dilated(xp, w[:, None, :].astype(x.dtype), (1,), 'VALID',
                                 dimension_numbers=('NWC', 'WIO', 'NWC'),
                                 feature_group_count=x.shape[-1])
    return y + b.astype(x.dtype), xp[:, -(w.shape[0] - 1):]


def _ssd_scan(x, dt, a, bm, cm, h0):
    bsz, t = x.shape[0], x.shape[1]
    pad = (-t) % CHUNK
    if pad:
        x = jnp.pad(x, ((0, 0), (0, pad), (0, 0), (0, 0)))
        dt = jnp.pad(dt, ((0, 0), (0, pad), (0, 0)))
        bm = jnp.pad(bm, ((0, 0), (0, pad), (0, 0), (0, 0)))
        cm = jnp.pad(cm, ((0, 0), (0, pad), (0, 0), (0, 0)))
    nc = (t + pad) // CHUNK
    G, E, P, N = SSD_GROUPS, SSD_HPG, SSD_HEAD_DIM, SSD_STATE
    xc = x.reshape(bsz, nc, CHUNK, G, E, P)
    dtc = dt.reshape(bsz, nc, CHUNK, G, E)
    bc = bm.reshape(bsz, nc, CHUNK, G, N)
    cc = cm.reshape(bsz, nc, CHUNK, G, N)
    xdt = xc * dtc[..., None]
    acs = jnp.moveaxis(jnp.cumsum(dtc * a.reshape(G, E), axis=2), 2, -1)
    idx = jnp.arange(CHUNK)
    causal = idx[:, None] >= idx[None, :]
    lmat = jnp.exp(jnp.where(causal, acs[..., :, None] - acs[..., None, :], -jnp.inf))
    cb = jnp.einsum('bclgn,bcsgn->bcgls', cc, bc)
    y_diag = jnp.einsum('bcgels,bcsgep->bclgep', cb[:, :, :, None] * lmat, xdt)
    decay_to_end = jnp.exp(acs[..., -1:] - acs)
    decay_from_start = jnp.exp(acs)
    chunk_decay = jnp.exp(acs[..., -1])

    def step(h, inp):
        b_c, c_c, xdt_c, dte_c, dfs_c, cd_c = inp
        y_off = jnp.einsum('blgn,bgepn,bgel->blgep', c_c, h, dfs_c)
        h = h * cd_c[..., None, None] + jnp.einsum('blgn,bgel,blgep->bgepn', b_c, dte_c, xdt_c)
        return h, y_off

    xs = (jnp.moveaxis(bc, 1, 0), jnp.moveaxis(cc, 1, 0), jnp.moveaxis(xdt, 1, 0),
          jnp.moveaxis(decay_to_end, 1, 0), jnp.moveaxis(decay_from_start, 1, 0),
          jnp.moveaxis(chunk_decay, 1, 0))
    h_fin, y_off = lax.scan(step, h0.reshape(bsz, G, E, P, N), xs)
    y = y_diag + jnp.moveaxis(y_off, 0, 1)
    y = y.reshape(bsz, nc * CHUNK, SSD_HEADS, P)[:, :t]
    return y, h_fin.reshape(bsz, SSD_HEADS, P, N)


def _mem_kv(mem, mem_norm_w, xa_kv_w):
    kv = _rmsnorm(mem, mem_norm_w) @ xa_kv_w
    k, v = jnp.split(kv, 2, axis=-1)
    b = mem.shape[0]
    return (k.reshape(b, N_MEM, XA_HEADS, XA_HEAD_DIM), v.reshape(b, N_MEM, XA_HEADS, XA_HEAD_DIM))


def _layer(x, mem_k, mem_v, st_a, st_ssd, st_h,
           norm_pre_w, w_in, gate_b, conv_dw_w, conv_dw_b, conv_ln_w, conv_ln_b, conv_out_w,
           ssd_conv_w, ssd_conv_b, ssd_dt_bias, ssd_a_log, ssd_d, ssd_norm_w, ssd_out_w,
           xa_out_w, w_out, norm_post_w):
    f32 = jnp.float32
    bsz, t = x.shape[0], x.shape[1]
    h = _rmsnorm(x, norm_pre_w)
    proj = h @ w_in
    points = []
    acc = 0
    for s in IN_SIZES[:-1]:
        acc += s
        points.append(acc)
    glu_v, glu_g, conv_gate, z, xbc, dt_raw, q, xa_gate, gates = jnp.split(proj, points, axis=-1)

    u = glu_v * jax.nn.sigmoid(glu_g)
    c, new_a = _causal_dwconv(u, st_a, conv_dw_w, conv_dw_b)
    c = jax.nn.silu(_layernorm(c, conv_ln_w, conv_ln_b)) * jax.nn.silu(conv_gate)
    out_a = c @ conv_out_w

    xbc_c, new_ssd = _causal_dwconv(xbc, st_ssd, ssd_conv_w, ssd_conv_b)
    xbc_c = jax.nn.silu(xbc_c)
    xs, bm, cm = jnp.split(xbc_c, [D_INNER, D_INNER + SSD_GROUPS * SSD_STATE], axis=-1)
    dt = jax.nn.softplus(dt_raw.astype(f32) + ssd_dt_bias.astype(f32))
    a = -jnp.exp(ssd_a_log.astype(f32))
    xh = xs.astype(f32).reshape(bsz, t, SSD_HEADS, SSD_HEAD_DIM)
    y, new_h = _ssd_scan(xh, dt, a,
                         bm.astype(f32).reshape(bsz, t, SSD_GROUPS, SSD_STATE),
                         cm.astype(f32).reshape(bsz, t, SSD_GROUPS, SSD_STATE),
                         st_h.astype(f32))
    y = y + xh * ssd_d.astype(f32)[:, None]
    y = y.reshape(bsz, t, D_INNER) * jax.nn.silu(z.astype(f32))
    yg = y.reshape(bsz, t, SSD_GROUPS, D_INNER // SSD_GROUPS)
    yg = yg * lax.rsqrt(jnp.mean(yg * yg, axis=-1, keepdims=True) + EPS)
    y = yg.reshape(bsz, t, D_INNER) * ssd_norm_w.astype(f32)
    out_b = y.astype(x.dtype) @ ssd_out_w

    qh = q.reshape(bsz, t, XA_HEADS, XA_HEAD_DIM).astype(f32)
    s = jnp.einsum('bthd,bmhd->bhtm', qh, mem_k.astype(f32)) * (XA_HEAD_DIM ** -0.5)
    p = jax.nn.softmax(s, axis=-1)
    o = jnp.einsum('bhtm,bmhd->bthd', p, mem_v.astype(f32)).reshape(bsz, t, XA_DIM).astype(x.dtype)
    out_c = (o * jax.nn.silu(xa_gate)) @ xa_out_w

    g = jax.nn.sigmoid(gates + gate_b).reshape(bsz, t, N_BRANCH, D_MODEL)
    m = g[:, :, 0] * out_a + g[:, :, 1] * out_b + g[:, :, 2] * out_c
    x_new = x + _rmsnorm(m @ w_out, norm_post_w)
    return x_new, new_a, new_ssd, new_h.astype(st_h.dtype)


def setup_inputs(seed: int = 0) -> dict:
    key = jax.random.key(seed)
    ks = iter(jax.random.split(key, 40))

    def nrm(shape, scale):
        return jax.random.normal(next(ks), shape, jnp.float32) * scale

    L = DEPTH
    x_prompt = nrm((BATCH, SEQ, D_MODEL), 1.0)
    x_sample = nrm((DEC_BATCH, DEC_SEQ, D_MODEL), 1.0)
    mem_prompt = nrm((BATCH, N_MEM, D_MODEL), 1.0)
    cache_mem_k = nrm((L, DEC_BATCH, N_MEM, XA_HEADS, XA_HEAD_DIM), 1.0)
    cache_mem_v = nrm((L, DEC_BATCH, N_MEM, XA_HEADS, XA_HEAD_DIM), 1.0)
    state_conv_a = nrm((L, DEC_BATCH, CONV_WIDTH - 1, CONV_DIM), 0.5)
    state_conv_ssd = nrm((L, DEC_BATCH, SSD_CONV_WIDTH - 1, SSD_XBC), 1.0)
    state_ssm = nrm((L, DEC_BATCH, SSD_HEADS, SSD_HEAD_DIM, SSD_STATE), 0.1)

    norm_pre_w = 1.0 + nrm((L, D_MODEL), 0.02)
    w_in = nrm((L, D_MODEL, N_IN), D_MODEL ** -0.5)
    gate_b = nrm((L, N_BRANCH * D_MODEL), 0.02)
    conv_dw_w = nrm((L, CONV_WIDTH, CONV_DIM), CONV_WIDTH ** -0.5)
    conv_dw_b = nrm((L, CONV_DIM), 0.02)
    conv_ln_w = 1.0 + nrm((L, CONV_DIM), 0.02)
    conv_ln_b = nrm((L, CONV_DIM), 0.02)
    conv_out_w = nrm((L, CONV_DIM, D_MODEL), CONV_DIM ** -0.5)
    ssd_conv_w = nrm((L, SSD_CONV_WIDTH, SSD_XBC), SSD_CONV_WIDTH ** -0.5)
    ssd_conv_b = nrm((L, SSD_XBC), 0.02)
    u = jax.random.uniform(next(ks), (L, SSD_HEADS), jnp.float32)
    dt0 = jnp.exp(u * (math.log(0.1) - math.log(0.001)) + math.log(0.001))
    ssd_dt_bias = dt0 + jnp.log(-jnp.expm1(-dt0))
    ssd_a_log = jnp.log(jax.random.uniform(next(ks), (L, SSD_HEADS), jnp.float32, 1.0, 16.0))
    ssd_d = 1.0 + nrm((L, SSD_HEADS), 0.1)
    ssd_norm_w = 1.0 + nrm((L, D_INNER), 0.02)
    ssd_out_w = nrm((L, D_INNER, D_MODEL), D_INNER ** -0.5)
    mem_norm_w = 1.0 + nrm((L, D_MODEL), 0.02)
    xa_kv_w = nrm((L, D_MODEL, 2 * XA_DIM), D_MODEL ** -0.5)
    xa_out_w = nrm((L, XA_DIM, D_MODEL), XA_DIM ** -0.5)
    w_out = nrm((L, D_MODEL, D_MODEL), D_MODEL ** -0.5)
    norm_post_w = 1.0 + nrm((L, D_MODEL), 0.02)
    return {'x_prompt': x_prompt, 'x_sample': x_sample, 'mem_prompt': mem_prompt,
            'cache_mem_k': cache_mem_k, 'cache_mem_v': cache_mem_v,
            'state_conv_a': state_conv_a, 'state_conv_ssd': state_conv_ssd, 'state_ssm': state_ssm,
            'norm_pre_w': norm_pre_w, 'w_in': w_in, 'gate_b': gate_b,
            'conv_dw_w': conv_dw_w, 'conv_dw_b': conv_dw_b, 'conv_ln_w': conv_ln_w,
            'conv_ln_b': conv_ln_b, 'conv_out_w': conv_out_w,
            'ssd_conv_w': ssd_conv_w, 'ssd_conv_b': ssd_conv_b, 'ssd_dt_bias': ssd_dt_bias,
            'ssd_a_log': ssd_a_log, 'ssd_d': ssd_d, 'ssd_norm_w': ssd_norm_w, 'ssd_out_w': ssd_out_w,
            'mem_norm_w': mem_norm_w, 'xa_kv_w': xa_kv_w, 'xa_out_w': xa_out_w,
            'w_out': w_out, 'norm_post_w': norm_post_w}


def reference(x_prompt, x_sample, mem_prompt, cache_mem_k, cache_mem_v,
              state_conv_a, state_conv_ssd, state_ssm,
              norm_pre_w, w_in, gate_b, conv_dw_w, conv_dw_b, conv_ln_w, conv_ln_b, conv_out_w,
              ssd_conv_w, ssd_conv_b, ssd_dt_bias, ssd_a_log, ssd_d, ssd_norm_w, ssd_out_w,
              mem_norm_w, xa_kv_w, xa_out_w, w_out, norm_post_w):
    def lw(l):
        return (norm_pre_w[l], w_in[l], gate_b[l], conv_dw_w[l], conv_dw_b[l], conv_ln_w[l],
                conv_ln_b[l], conv_out_w[l], ssd_conv_w[l], ssd_conv_b[l], ssd_dt_bias[l],
                ssd_a_log[l], ssd_d[l], ssd_norm_w[l], ssd_out_w[l], xa_out_w[l], w_out[l],
                norm_post_w[l])

    bp = x_prompt.shape[0]
    dtp = x_prompt.dtype
    y_p = x_prompt
    mk_p, mv_p, ca_p, cs_p, hs_p = [], [], [], [], []
    for l in range(DEPTH):
        mk, mv = _mem_kv(mem_prompt, mem_norm_w[l], xa_kv_w[l])
        y_p, na, ns, nh = _layer(y_p, mk, mv,
                                 jnp.zeros((bp, CONV_WIDTH - 1, CONV_DIM), dtp),
                                 jnp.zeros((bp, SSD_CONV_WIDTH - 1, SSD_XBC), dtp),
                                 jnp.zeros((bp, SSD_HEADS, SSD_HEAD_DIM, SSD_STATE), dtp),
                                 *lw(l))
        mk_p.append(mk); mv_p.append(mv); ca_p.append(na); cs_p.append(ns); hs_p.append(nh)

    y_s = x_sample
    ca_s, cs_s, hs_s = [], [], []
    for l in range(DEPTH):
        y_s, na, ns, nh = _layer(y_s, cache_mem_k[l], cache_mem_v[l],
                                 state_conv_a[l], state_conv_ssd[l], state_ssm[l], *lw(l))
        ca_s.append(na); cs_s.append(ns); hs_s.append(nh)

    return (y_p, y_s,
            jnp.stack(mk_p), jnp.stack(mv_p), jnp.stack(ca_p), jnp.stack(cs_p), jnp.stack(hs_p),
            jnp.stack(ca_s), jnp.stack(cs_s), jnp.stack(hs_s))
```

```python
import numpy as np
from contextlib import ExitStack
import concourse.bass as bass
import concourse.mybir as mybir
from concourse.bass_utils import run_bass_kernel_spmd

F32 = mybir.dt.float32
BF16 = mybir.dt.bfloat16
AF = mybir.ActivationFunctionType
ALU = mybir.AluOpType

D = 1024
KC = 8
NS = 51
SLOTW = 4096
EPS = 1e-6
import os
SAME_ENG_WAIT = os.environ.get("SAME_ENG_WAIT", "1") == "1"
DBG_STOP = os.environ.get('DBG_STOP', '')


class _Stop(Exception):
    pass


STOPPED = [False]
LT_COUNT = [0]


PHASES = []
PE_OPS = [0]


def stage(name):
    PHASES.append((LT_COUNT[0], name, PE_OPS[0]))
    if DBG_STOP == name and not STOPPED[0]:
        STOPPED[0] = True
        print('STOPPED at', name)

S_GV = (0, 2); S_GG = (1, 3)
S_DA = 4
S_CG = (12, 13)
S_G0 = (14, 15); S_CO = (16, 17)
S_Z = 18
S_XBC = ((22, 23), (25, 26), (28, 29)); S_DS = (24, 27, 30)
S_G1 = (31, 32); S_SO = (33, 34, 35, 36)
S_Q = (37, 38); S_XG = (39, 40)
S_G2 = (41, 42); S_XO = (43, 44)
S_WO = (45, 46)
S_KVK = (47, 48); S_KVV = (49, 50)
NS_MAIN = 47

PC_NPRE = 0; PC_GB = 8; PC_DWB = 32; PC_LNW = 40; PC_LNB = 48; PC_SCB = 56
PC_D = 80; PC_SNW = 96; PC_NPOST = 112; PC_MNW = 120


class Reg:
    __slots__ = ("w", "r", "excl")

    def __init__(self, init=None, excl=False):
        self.w = dict(init) if init else {}
        self.r = {}
        self.excl = excl


class Eng:
    def __init__(self, name, h, sem):
        self.name = name; self.h = h; self.sem = sem; self.cnt = 0
        self.key = "e_" + name
        self.seen = {}


class DSlot:
    def __init__(self, key, sem):
        self.key = key; self.sem = sem; self.cnt = 0


class Buf:
    def __init__(self, t, regs):
        self.t = t; self.regs = regs

    def __getitem__(self, idx):
        return self.t[idx]


class KB:
    def __init__(self, nc, es):
        self.nc = nc; self.es = es
        self.E = {}
        for name, h in (("pe", nc.tensor), ("act", nc.scalar), ("dve", nc.vector),
                        ("pool", nc.gpsimd), ("sp", nc.sync)):
            sem = es.enter_context(nc.semaphore("sem_" + name))
            self.E[name] = Eng(name, h, sem)
        self.grave = {}
        self.uid = 0
        self.dslots = []
        self.nwait = 0
        self.ninst = 0

    def name(self, p):
        self.uid += 1
        return "%s%d" % (p, self.uid)

    def dslot(self):
        sem = self.es.enter_context(self.nc.semaphore(self.name("dsem")))
        s = DSlot(self.name("d"), sem)
        self.dslots.append(s)
        return s

    def init_arena(self, nbytes):
        self.ARENA = nbytes
        self.arena = self.es.enter_context(self.nc.sbuf_tensor("arena", [128, nbytes // 2], BF16))
        self.ablk = [Reg() for _ in range(nbytes // 1024 + 2)]
        self.atop = 0
        self.apeak = 0

    def sb(self, shape, dt, nreg=1, es=None, own_block=False):
        if es is not None:
            esz = 4 if dt == F32 else 2
            per = 1
            for d_ in shape[1:]:
                per *= d_
            nb = per * esz
            off = (self.atop + 1023) // 1024 * 1024 if (nb >= 1024 or own_block) else (self.atop + 63) // 64 * 64
            assert off + nb <= self.ARENA, ("arena overflow", off, nb, self.ARENA)
            prev = self.atop
            self.atop = off + (max(nb, 1024) if own_block else nb)
            self.apeak = max(self.apeak, self.atop)

            def _rel(prev=prev):
                self.atop = prev
            es.callback(_rel)
            ap = self.arena[:, off // 2:(off + nb) // 2]
            if dt == F32:
                ap = ap.bitcast(F32)
            if len(shape) == 3:
                ap = ap.rearrange("p (a b) -> p a b", a=shape[1])
            elif len(shape) == 4:
                ap = ap.rearrange("p (a b c) -> p a b c", a=shape[1], b=shape[2])
            regs = []
            for i in range(nreg):
                b0 = off + i * nb // nreg
                b1 = off + (i + 1) * nb // nreg
                regs.append([self.ablk[j] for j in range(b0 // 1024, (b1 - 1) // 1024 + 1)])
            return Buf(ap, regs)
        nm = self.name("sb")
        if os.environ.get('DBG_ALLOC'):
            print('alloc', nm, shape, dt, 'scoped' if es is not None else 'persist')
        t = (es if es is not None else self.es).enter_context(self.nc.sbuf_tensor(nm, list(shape), dt))
        return Buf(t, [Reg(self.grave) for _ in range(nreg)])

    def psum(self, shape, dt, nreg=1):
        t = self.es.enter_context(self.nc.psum_tensor(self.name("ps"), list(shape), dt))
        return Buf(t, [Reg(excl=True) for _ in range(nreg)])

    def bury(self, bufs):
        return
        for b in bufs:
            for r in b.regs:
                for d in (r.w, r.r):
                    for k, v in d.items():
                        if k not in self.grave or self.grave[k][1] < v[1]:
                            self.grave[k] = v

    @staticmethod
    def flat(regs):
        out = []
        for r in regs:
            if isinstance(r, (list, tuple)):
                out.extend(KB.flat(r))
            else:
                out.append(r)
        return out

    def _waits(self, e, reads, writes):
        deps = {}
        for r in reads:
            for k, v in r.w.items():
                if k not in deps or deps[k][1] < v[1]:
                    deps[k] = v
            if r.excl:
                for k, v in r.r.items():
                    if k != e.key and (k not in deps or deps[k][1] < v[1]):
                        deps[k] = v
        for w in writes:
            for d in (w.w, w.r):
                for k, v in d.items():
                    if k not in deps or deps[k][1] < v[1]:
                        deps[k] = v
        for k, (sem, cnt) in deps.items():
            if k == e.key and (e.name == "pe" or not SAME_ENG_WAIT):
                continue
            if e.seen.get(k, 0) >= cnt:
                continue
            e.h.wait_ge(sem, cnt)
            e.seen[k] = cnt
            self.nwait += 1
            if os.environ.get('DBG_WAITS'):
                print('  wait', e.name, 'on', k, cnt)

    def _mark(self, ev, reads, writes):
        k, sem, cnt = ev
        for r in reads:
            if k not in r.r or r.r[k][1] < cnt:
                r.r[k] = (sem, cnt)
        for w in writes:
            w.w = {k: (sem, cnt)}
            w.r = {}

    def op(self, eng, fn, reads=(), writes=(), inc=True):
        if STOPPED[0]:
            return
        e = self.E[eng]
        reads = self.flat(reads); writes = self.flat(writes)
        self._waits(e, reads, writes)
        ins = fn(e.h)
        self.ninst += 1
        if eng == 'pe':
            PE_OPS[0] += 1
        if os.environ.get('DBG_WAITS'):
            print('op', eng, self.ninst, ('inc->%d' % (e.cnt + 1)) if inc else '')
        if inc:
            e.cnt += 1
            ins.then_inc(e.sem, 1)
            ev = (e.key, e.sem, e.cnt)
        else:
            ev = (e.key, e.sem, e.cnt + 1)
        self._mark(ev, reads, writes)

    def dma(self, q, out, in_, slot, reads=(), writes=(), **kw):
        if STOPPED[0]:
            return
        e = self.E[q]
        reads = self.flat(reads); writes = self.flat(writes)
        self._waits(e, reads, writes)
        if slot.cnt > 0 and e.seen.get(slot.key, 0) < slot.cnt:
            e.h.wait_ge(slot.sem, slot.cnt)
            e.seen[slot.key] = slot.cnt
        ins = e.h.dma_start(out=out, in_=in_, **kw)
        slot.cnt += 16
        ins.then_inc(slot.sem, 16)
        self.ninst += 1
        self._mark((slot.key, slot.sem, slot.cnt), reads, writes)

    def finish(self):
        e = self.E["sp"]
        for s in self.dslots:
            if s.cnt > 0 and e.seen.get(s.key, 0) < s.cnt:
                e.h.wait_ge(s.sem, s.cnt)
                e.seen[s.key] = s.cnt


def bc_last(ap, n):
    sh = list(ap.shape)
    return ap.unsqueeze(len(sh)).broadcast_to(sh + [n])


class Seq:
    pass


def build(T_PROMPT=4096, TT_P=512, Q_P=128, N_SAMP=2, T_S=64):
    STOPPED[0] = False
    LT_COUNT[0] = 0
    nc = bass.Bass("TRN2", target_bir_lowering=False)
    NT_P = T_PROMPT // TT_P
    def din(name, shape, dt=F32):
        return nc.dram_tensor(name, list(shape), dt, kind="ExternalInput").ap()

    def dout(name, shape, dt=F32):
        return nc.dram_tensor(name, list(shape), dt, kind="ExternalOutput").ap()

    ws = din("ws", [2, NS, 128, SLOTW])
    kvc = din("kvc", [N_SAMP, 2, 128, SLOTW])
    wdt_d = din("wdt", [2, 128, KC * 32])
    pcol_d = din("pcol", [128, 256])
    prow_d = din("prow", [128, 128])
    cst_d = din("cst", [128, 512])
    xp_d = din("xp", [128, KC, T_PROMPT])
    xs_d = din("xs", [N_SAMP, 128, KC, T_S])
    mem_d = din("memT", [128, KC, 256])
    sca_d = din("sca", [N_SAMP, 2, 128, KC, 30])
    scs_d = din("scs", [N_SAMP, 2, 128, 24, 3])
    ssm_d = din("ssm", [N_SAMP, 2, 128, 2048])

    yp_o = dout("yp", [128, KC, T_PROMPT])
    ys_o = dout("ys", [N_SAMP, 128, KC, T_S])
    mk_o = dout("mk", [2, 128, 2048])
    mv_o = dout("mv", [2, 128, 2048])
    cap_o = dout("cap", [2, 128, KC, 30])
    csp_o = dout("csp", [2, 128, 24, 3])
    hp_o = dout("hp", [2, 128, 2048])
    cas_o = dout("cas", [N_SAMP, 2, 128, KC, 30])
    css_o = dout("css", [N_SAMP, 2, 128, 24, 3])
    hs_o = dout("hs", [N_SAMP, 2, 128, 2048])

    wsb = nc.dram_tensor("wsb", [2, NS, 128, SLOTW], BF16).ap()
    kvb = nc.dram_tensor("kvb", [1 + N_SAMP, 2, 128, SLOTW], BF16).ap()
    wsb_reg = [[Reg() for _ in range(NS)] for _ in range(2)]
    kvb_reg = [[Reg() for _ in range(2)] for _ in range(1 + N_SAMP)]

    with ExitStack() as es:
        k = KB(nc, es)
        k.init_arena(int(os.environ.get('ARENA_KB', '72')) * 1024)
        op = k.op

        cstf = k.sb([128, 512], F32)
        identb = k.sb([128, 128], BF16)
        masknegb = k.sb([128, 128], BF16)
        onesb = k.sb([128, 128], BF16)
        trib = k.sb([128, 128], BF16)
        pcol = k.sb([128, 256], F32)
        prow = k.sb([128, 128], F32)
        arow = k.sb([128, 64], F32)
        es0 = ExitStack()
        wdtf = k.sb([128, 2 * KC * 32], F32, es=es0)
        wdtb = k.sb([128, 2, KC, 32], BF16)
        ld0 = k.dslot()
        k.dma("pool", cstf[:, :], cst_d, ld0, writes=cstf.regs)
        k.dma("pool", pcol[:, :], pcol_d, ld0, writes=pcol.regs)
        k.dma("pool", prow[:, :], prow_d, ld0, writes=prow.regs)
        k.dma("pool", wdtf[:, 0:256], wdt_d[0], ld0, writes=wdtf.regs)
        k.dma("pool", wdtf[:, 256:512], wdt_d[1], ld0, writes=wdtf.regs)
        identf = cstf[:, 0:128]
        trif = cstf[:, 128:256]
        onesf = cstf[:, 384:512]
        op("act", lambda h: h.activation(out=identb[:, :], in_=cstf[:, 0:128], func=AF.Copy),
           reads=cstf.regs, writes=identb.regs)
        op("act", lambda h: h.activation(out=masknegb[:, :], in_=cstf[:, 256:384], func=AF.Copy),
           reads=cstf.regs, writes=masknegb.regs)
        op("act", lambda h: h.activation(out=onesb[:, :], in_=cstf[:, 384:512], func=AF.Copy),
           reads=cstf.regs, writes=onesb.regs)
        op("act", lambda h: h.activation(out=trib[:, :], in_=cstf[:, 128:256], func=AF.Copy),
           reads=cstf.regs, writes=trib.regs)
        op("act", lambda h: h.activation(out=wdtb[:, :, :, :].rearrange("p a b c -> p (a b c)"), in_=wdtf[:, :], func=AF.Copy),
           reads=wdtf.regs, writes=wdtb.regs)
        es0.close()
        for l in range(2):
            op("act", lambda h: h.activation(out=arow[:, l * 32:(l + 1) * 32], in_=prow[:, l * 64 + 32:l * 64 + 64], func=AF.Exp),
               reads=prow.regs, writes=arow.regs)
        op("dve", lambda h: h.tensor_scalar(out=arow[:, :], in0=arow[:, :], scalar1=-1.0, scalar2=None, op0=ALU.mult),
           reads=arow.regs, writes=arow.regs)

        TT = TT_P
        NRING = 7
        LOOKAHEAD = 3
        ring = [k.sb([128, SLOTW], BF16) for _ in range(NRING)]
        ring_sl = [k.dslot() for _ in range(NRING)]
        ring_i = [0]

        def load_slot(src_ap, src_reg):
            i = ring_i[0] % NRING
            ring_i[0] += 1
            k.dma("sp", ring[i][:, :], src_ap, ring_sl[i], reads=[src_reg], writes=ring[i].regs)
            return ring[i]

        cast_done = [[False] * NS for _ in range(2)]
        sto_sl = [k.dslot() for _ in range(2)]

        def cast_into(dst, src_ap, slot):
            k.dma("pool", dst[:, :], src_ap, slot, writes=dst.regs, max_dma_last_dim=8192)

        prepared = {}

        def prepare(l, s):
            i = ring_i[0] % NRING
            ring_i[0] += 1
            r = ring[i]
            if cast_done[l][s]:
                if os.environ.get('DBG_WAITS'):
                    print('RINGLOAD', l, s, 'ring', i)
                k.dma("sp", r[:, :], wsb[l, s], ring_sl[i], reads=[wsb_reg[l][s]], writes=r.regs)
                return r
            cast_done[l][s] = True
            cast_into(r, ws[l, s], ring_sl[i])
            if s < NS_MAIN:
                k.dma("sp", wsb[l, s], r[:, :], sto_sl[s % 2], reads=r.regs, writes=[wsb_reg[l][s]])
            return r

        def wslot(l, s):
            r = prepared.pop((l, s), None)
            if r is None:
                r = prepare(l, s)
            if s < NS_MAIN:
                nxt = (l, s)
                for _ in range(LOOKAHEAD):
                    nxt = (nxt[0], nxt[1] + 1) if nxt[1] + 1 < NS_MAIN else (1 - nxt[0], 0)
                    if nxt not in prepared:
                        prepared[nxt] = prepare(*nxt)
            return r

        bufA = k.sb([128, KC, TT], F32, KC)
        hT = k.sb([128, KC, TT], BF16, KC)
        bufB = k.sb([128, KC, TT], F32, KC)
        BUFS = {'x': bufA, 'm': bufB}
        GS = {'preloaded': False, 'prenorm_done': False}
        hst = [k.sb([128, 2048], F32, 4) for _ in range(2)]
        hstb = k.sb([128, 2048], BF16, 4)
        halo_a = [k.sb([128, KC, 30], BF16) for _ in range(2)]
        halo_s = [k.sb([128, 24, 3], BF16) for _ in range(2)]
        ulast = [k.sb([128, KC, 30], F32) for _ in range(2)]
        xlast = [k.sb([128, 24, 3], F32) for _ in range(2)]
        kv_sl = k.dslot()
        io_sl = [k.dslot() for _ in range(4)]
        io_i = [0]

        def ioslot():
            io_i[0] += 1
            return io_sl[io_i[0] % 4]

        pA = [k.psum([128, 512], F32) for _ in range(2)]
        pB = [k.psum([128, 512], F32) for _ in range(2)]
        pC = k.psum([128, 512], F32)
        pDs = [k.psum([128, 512], F32) for _ in range(2)]
        pD = pDs[0]
        pT = [k.psum([128, 1024], BF16) for _ in range(1)]
        rot = {"A": 0, "B": 0, "T": 0}

        def psA():
            rot["A"] += 1
            return pA[rot["A"] % 2]

        def psB():
            rot["B"] += 1
            return pB[rot["B"] % 2]

        def psT():
            return pT[0]

        tf = [k.sb([128, TT], F32) for _ in range(3)]
        tfi = [0]

        def tmpf():
            tfi[0] += 1
            return tf[tfi[0] % 3]

        tb = [k.sb([128, TT], BF16) for _ in range(3)]
        tbi = [0]

        def tmpb():
            tbi[0] += 1
            return tb[tbi[0] % 3]

        def mm_group(ps_ap, ps_reg, pairs, extra_reads=()):
            n = len(pairs)
            for i, (l_ap, r_ap, regs) in enumerate(pairs):
                op("pe", lambda h: h.matmul(ps_ap, lhsT=l_ap, rhs=r_ap, start=(i == 0), stop=(i == n - 1)),
                   reads=list(regs), writes=[ps_reg], inc=(i == n - 1))

        def proj(slot, blk, rhsbuf, N, ps, nk=KC, bw=512):
            sv = slot[:, :].rearrange("p (k c) -> p k c", k=nk)
            pairs = [(sv[:, kc, blk * 128:(blk + 1) * 128], rhsbuf[:, kc, 0:N], [slot.regs[0], rhsbuf.regs[kc]])
                     for kc in range(nk)]
            mm_group(ps[:, 0:N], ps.regs[0], pairs)

        def act(out, in_, func, reads, writes, bias=None, scale=None):
            kw = {}
            if bias is not None:
                kw["bias"] = bias
            if scale is not None:
                kw["scale"] = scale
            op("act", lambda h: h.activation(out=out, in_=in_, func=func, **kw), reads=reads, writes=writes)

        def tt(out, in0, in1, o, reads, writes, eng="dve"):
            op(eng, lambda h: h.tensor_tensor(out=out, in0=in0, in1=in1, op=o), reads=reads, writes=writes)

        def stt(out, in0, scalar, in1, op0, op1, reads, writes):
            op("dve", lambda h: h.scalar_tensor_tensor(out=out, in0=in0, scalar=scalar, in1=in1, op0=op0, op1=op1),
               reads=reads, writes=writes)

        def ts(out, in0, s1, s2, op0, op1, reads, writes):
            if op1 is None:
                op("dve", lambda h: h.tensor_scalar(out=out, in0=in0, scalar1=s1, scalar2=None, op0=op0), reads=reads, writes=writes)
            else:
                op("dve", lambda h: h.tensor_scalar(out=out, in0=in0, scalar1=s1, scalar2=s2, op0=op0, op1=op1), reads=reads, writes=writes)

        def rstd_from(ps, N, scale, dst):
            act(dst[:, 0:N], ps[:, 0:N], AF.Ln, ps.regs, dst.regs, bias=epsc[:, 0:1], scale=scale)
            act(dst[:, 0:N], dst[:, 0:N], AF.Exp, dst.regs, dst.regs, scale=-0.5)

        epsc = k.sb([128, 1], F32)
        op("dve", lambda h: h.memset(epsc[:, :], EPS), writes=epsc.regs)

        def pc(l, col):
            return pcol[:, l * 128 + col:l * 128 + col + 1]

        def prenorm(l, N, xb):
            with ExitStack() as esq:
                sqb = k.sb([128, KC, TT], BF16, KC, es=esq)
                for kc in range(KC):
                    act(sqb[:, kc, 0:N], xb[:, kc, 0:N], AF.Square, [xb.regs[kc]], [sqb.regs[kc]])
                ps = pC
                mm_group(ps[:, 0:N], ps.regs[0], [(onesb[:, :], sqb[:, kc, 0:N], [onesb.regs[0], sqb.regs[kc]]) for kc in range(KC)])
            rs = tmpf()
            rstd_from(ps, N, 1.0 / D, rs)
            for kc in range(KC):
                stt(hT[:, kc, 0:N], xb[:, kc, 0:N], pc(l, PC_NPRE + kc), rs[:, 0:N], ALU.mult, ALU.mult,
                    [xb.regs[kc], rs.regs[0], pcol.regs[0]], [hT.regs[kc]])

        def layer_tile(S, l, N, Q, first, last):
            nch = N // Q
            if getattr(S, 'prenorm_done', False):
                S.prenorm_done = False
            else:
                prenorm(l, N, BUFS['x'])

            stage('prenorm')
            with ExitStack() as esA:
                ubuf = k.sb([128, KC, 30 + TT], BF16, KC, es=esA)
                cc = k.sb([128, KC, TT], F32, KC, es=esA)
                ccb = k.sb([128, KC, TT], BF16, KC, es=esA)
                scg = k.sb([128, KC, TT], BF16, KC, es=esA)
                stA = [k.sb([128, TT], F32, es=esA) for _ in range(4)]
                op("act", lambda h: h.activation(out=ubuf[:, :, 0:30], in_=halo_a[l][:, :, :], func=AF.Copy),
                   reads=halo_a[l].regs, writes=ubuf.regs)
                for half in range(2):
                    sv = wslot(l, S_GV[half]); sg_ = wslot(l, S_GG[half])
                    for b in range(4):
                        c = half * 4 + b
                        p1 = psA(); p2 = psB()
                        proj(sv, b, hT, N, p1)
                        proj(sg_, b, hT, N, p2)
                        sgm = tmpf()
                        act(sgm[:, 0:N], p2[:, 0:N], AF.Sigmoid, p2.regs, sgm.regs)
                        tt(ubuf[:, c, 30:30 + N], p1[:, 0:N], sgm[:, 0:N], ALU.mult, p1.regs + sgm.regs, [ubuf.regs[c]])
                        if last:
                            tt(ulast[l][:, c, :], p1[:, N - 30:N], sgm[:, N - 30:N], ALU.mult, p1.regs + sgm.regs, ulast[l].regs)
                op("act", lambda h: h.activation(out=halo_a[l][:, :, :], in_=ubuf[:, :, N:N + 30], func=AF.Copy),
                   reads=ubuf.regs, writes=halo_a[l].regs)
                pend_stats = []
                for c in range(KC):
                    sd = wslot(l, S_DA + c)
                    dv = sd[:, :].rearrange("p (w j) -> p w j", w=32)
                    p1 = psA()
                    mm_group(p1[:, 0:N], p1.regs[0],
                             [(dv[:, w, :], ubuf[:, c, w:w + N], [sd.regs[0], ubuf.regs[c]]) for w in range(31)])
                    ts(cc[:, c, 0:N], p1[:, 0:N], pc(l, PC_DWB + c), None, ALU.add, None, p1.regs + pcol.regs, [cc.regs[c]])
                    act(ccb[:, c, 0:N], p1[:, 0:N], AF.Identity, p1.regs + pcol.regs, [ccb.regs[c]], bias=pc(l, PC_DWB + c))
                    sq = tmpb()
                    act(sq[:, 0:N], p1[:, 0:N], AF.Square, p1.regs + pcol.regs, sq.regs, bias=pc(l, PC_DWB + c))
                    def _stats(c=c, sq=sq):
                        op("pe", lambda h: h.matmul(pC[:, 0:N], lhsT=onesb[:, :], rhs=ccb[:, c, 0:N], start=(c == 0), stop=(c == KC - 1)),
                           reads=[onesb.regs[0], ccb.regs[c]], writes=pC.regs, inc=True)
                        op("pe", lambda h: h.matmul(pD[:, 0:N], lhsT=onesb[:, :], rhs=sq[:, 0:N], start=(c == 0), stop=(c == KC - 1)),
                           reads=[onesb.regs[0], sq.regs[0]], writes=pD.regs, inc=True)
                    if pend_stats:
                        pend_stats.pop()()
                    pend_stats.append(_stats)
                pend_stats.pop()()
                mean, msq, Ai, Bm = stA
                ts(mean[:, 0:N], pC[:, 0:N], 1.0 / D, None, ALU.mult, None, pC.regs, mean.regs)
                tt(msq[:, 0:N], mean[:, 0:N], mean[:, 0:N], ALU.mult, mean.regs, msq.regs)
                stt(msq[:, 0:N], pD[:, 0:N], 1.0 / D, msq[:, 0:N], ALU.mult, ALU.subtract, pD.regs + msq.regs, msq.regs)
                act(Ai[:, 0:N], msq[:, 0:N], AF.Ln, msq.regs, Ai.regs, bias=epsc[:, 0:1])
                act(Ai[:, 0:N], Ai[:, 0:N], AF.Exp, Ai.regs, Ai.regs, scale=-0.5)
                stt(Bm[:, 0:N], mean[:, 0:N], -1.0, Ai[:, 0:N], ALU.mult, ALU.mult, mean.regs + Ai.regs, Bm.regs)
                for half in range(2):
                    sw = wslot(l, S_CG[half])
                    for b in range(4):
                        c = half * 4 + b
                        p1 = psA()
                        proj(sw, b, hT, N, p1)
                        act(scg[:, c, 0:N], p1[:, 0:N], AF.Silu, p1.regs, [scg.regs[c]])
                t3s = {}
                for c in range(KC + 1):
                    if c < KC:
                        tt(cc[:, c, 0:N], cc[:, c, 0:N], Ai[:, 0:N], ALU.mult, [cc.regs[c]] + Ai.regs, [cc.regs[c]])
                        tt(cc[:, c, 0:N], cc[:, c, 0:N], Bm[:, 0:N], ALU.add, [cc.regs[c]] + Bm.regs, [cc.regs[c]])
                        t3 = tmpb()
                        t3s[c] = t3
                        act(t3[:, 0:N], cc[:, c, 0:N], AF.Silu, [cc.regs[c]] + pcol.regs, t3.regs,
                            bias=pc(l, PC_LNB + c), scale=pc(l, PC_LNW + c))
                    if c >= 1:
                        t3 = t3s.pop(c - 1)
                        tt(scg[:, c - 1, 0:N], t3[:, 0:N], scg[:, c - 1, 0:N], ALU.mult, t3.regs + [scg.regs[c - 1]], [scg.regs[c - 1]])
                gbufa = k.sb([128, KC, TT], F32, KC, es=esA)
                for gi in range(2):
                    gsl = wslot(l, S_G0[gi])
                    for bb in range(4):
                        c = gi * 4 + bb
                        p2 = psB()
                        proj(gsl, bb, hT, N, p2)
                        act(gbufa[:, c, 0:N], p2[:, 0:N], AF.Sigmoid, p2.regs + pcol.regs, [gbufa.regs[c]], bias=pc(l, PC_GB + c))
                for oi in range(2):
                    so = wslot(l, S_CO[oi])
                    for bb in range(4):
                        c = oi * 4 + bb
                        p1 = psA()
                        proj(so, bb, scg, N, p1)
                        tt(BUFS['m'][:, c, 0:N], p1[:, 0:N], gbufa[:, c, 0:N], ALU.mult, p1.regs + [gbufa.regs[c]], [BUFS['m'].regs[c]])
                k.bury([ubuf, cc, ccb, scg] + stA)

            stage('A')
            with ExitStack() as esB:
                sz = k.sb([128, 16, TT], BF16, 16, es=esB)
                NW = nch * 32
                dts = k.sb([128, 8, NW], F32, 1, es=esB)
                cdt = k.sb([128, NW], F32, 1, es=esB, own_block=True)
                dab16 = k.sb([128, NW], BF16, 1, es=esB, own_block=True)
                dal16 = k.sb([128, NW], BF16, 1, es=esB, own_block=True)
                eacs = k.sb([128, NW], F32, 1, es=esB, own_block=True)
                esXC = ExitStack()
                xc = k.sb([128, 24, TT], BF16, 24, es=esXC)
                esX = ExitStack()
                xbuf = k.sb([128, 8, 3 + TT], BF16, 8, es=esX)
                dr = dts.regs
                v_, ax, ee, dt_, da, nacs, dte, dd = [dts[:, i, :] for i in range(8)]
                pd = psB()

                def v3(ap):
                    return ap.rearrange("p (c h) -> p c h", c=nch)

                def bc3(ap):
                    return ap.unsqueeze(1).broadcast_to([Q, nch, 32])

                def dt_part1():
                    for qi in range(nch):
                        cs = slice(qi * Q, (qi + 1) * Q)
                        mm_group(pd[0:Q, qi * 32:(qi + 1) * 32], pd.regs[0],
                                 [(hT[:, kc, cs], wdtb[:, l, kc, :], [hT.regs[kc], wdtb.regs[0]]) for kc in range(KC)])

                def dt_chain1():
                    tt(v3(v_[0:Q]), v3(pd[0:Q, 0:NW]), bc3(prow[0:Q, l * 64:l * 64 + 32]), ALU.add, [pd.regs[0]] + prow.regs, dr)
                    ts(ax[0:Q], v_[0:Q], -1.0, None, ALU.mult, None, dr, dr)
                    tt(ax[0:Q], ax[0:Q], v_[0:Q], ALU.max, dr, dr)
                    act(ee[0:Q], ax[0:Q], AF.Exp, dr, dr, scale=-1.0)
                    act(ee[0:Q], ee[0:Q], AF.Ln, dr, dr, bias=1.0)
                    stt(dt_[0:Q], v_[0:Q], 0.0, ee[0:Q], ALU.max, ALU.add, dr, dr)
                    tt(v3(da[0:Q]), v3(dt_[0:Q]), bc3(arow[0:Q, l * 32:(l + 1) * 32]), ALU.mult, dr + arow.regs, dr)
                    act(dab16[0:Q, :], da[0:Q], AF.Copy, dr, dab16.regs)
                    tt(ee[0:Q], da[0:Q], dab16[0:Q, :], ALU.subtract, dr + dab16.regs, dr)
                    act(dal16[0:Q, :], ee[0:Q], AF.Copy, dr, dal16.regs)
                    mm_group(pd[0:Q, 128:128 + NW], pd.regs[0], [(trif[0:Q, 0:Q], da[0:Q], dr + cstf.regs)])
                    mm_group(pd[0:128, 256:256 + NW], pd.regs[0], [(onesf[0:Q, 0:128], da[0:Q], dr + cstf.regs)])

                def dt_chain2():
                    ts(nacs[0:Q], pd[0:Q, 128:128 + NW], -1.0, None, ALU.mult, None, [pd.regs[0]], dr)
                    act(eacs[0:Q, :], nacs[0:Q], AF.Exp, dr, eacs.regs, scale=-1.0)
                    tt(dd[0:Q], pd[0:Q, 256:256 + NW], nacs[0:Q], ALU.add, [pd.regs[0]] + dr, dr)
                    act(dte[0:Q], dd[0:Q], AF.Exp, dr, dr)
                    act(cdt[:, :], pd[0:128, 256:256 + NW], AF.Exp, [pd.regs[0]], cdt.regs)
                    tt(dte[0:Q], dte[0:Q], dt_[0:Q], ALU.mult, dr, dr)
                    ts(ax[0:Q], dt_[0:Q], 1e-30, None, ALU.max, None, dr, dr)
                    act(ax[0:Q], ax[0:Q], AF.Ln, dr, dr)
                    tt(nacs[0:Q], nacs[0:Q], ax[0:Q], ALU.add, dr, dr)

                dt_part1()
                for zi in range(4):
                    sw = wslot(l, S_Z + zi)
                    for b in range(4):
                        j = zi * 4 + b
                        p1 = psA()
                        proj(sw, b, hT, N, p1)
                        act(sz[:, j, 0:N], p1[:, 0:N], AF.Silu, p1.regs, [sz.regs[j]])
                    if zi == 1:
                        dt_chain1()
                    if zi == 3:
                        dt_chain2()
                for gi in range(3):
                    op("act", lambda h: h.activation(out=xbuf[:, :, 0:3], in_=halo_s[l][:, gi * 8:(gi + 1) * 8, :], func=AF.Copy),
                       reads=halo_s[l].regs, writes=xbuf.regs)
                    for half in range(2):
                        sw = wslot(l, S_XBC[gi][half])
                        for b in range(4):
                            jj = half * 4 + b
                            p1 = psA()
                            proj(sw, b, hT, N, p1)
                            act(xbuf[:, jj, 3:3 + N], p1[:, 0:N], AF.Copy, p1.regs, [xbuf.regs[jj]])
                            if last:
                                op("dve", lambda h: h.tensor_copy(out=xlast[l][:, gi * 8 + jj, :], in_=p1[:, N - 3:N]),
                                   reads=p1.regs, writes=xlast[l].regs)
                    op("act", lambda h: h.activation(out=halo_s[l][:, gi * 8:(gi + 1) * 8, :], in_=xbuf[:, :, N:N + 3], func=AF.Copy),
                       reads=xbuf.regs, writes=halo_s[l].regs)
                    sd = wslot(l, S_DS[gi])
                    dv = sd[:, :].rearrange("p (j w c) -> p j w c", j=8, w=4)
                    for jj in range(8):
                        j = gi * 8 + jj
                        p1 = psB()
                        mm_group(p1[:, 0:N], p1.regs[0],
                                 [(dv[:, jj, w, :], xbuf[:, jj, w:w + N], [sd.regs[0], xbuf.regs[jj]]) for w in range(4)])
                        act(xc[:, j, 0:N], p1[:, 0:N], AF.Silu, p1.regs + pcol.regs, [xc.regs[j]], bias=pc(l, PC_SCB + j))
                k.bury([xbuf])
                esX.close()
                stage('Bconv')
                esS = ExitStack()
                yoff = k.sb([128, 2048], BF16, 4, es=esS)
                xtm = k.sb([128, 2048], BF16, 2, es=esS)
                xdt2 = k.sb([128, 2048], BF16, 2, es=esS)
                btm = k.sb([128, 512], BF16, es=esS)
                LTs = [k.sb([128, 128], BF16, es=esS, own_block=True) for _ in range(3)]
                MT = k.sb([128, 32, 128], BF16, 32, es=esS)
                for qd in range(4):
                    act(hstb[:, qd * 512:(qd + 1) * 512], hst[l][:, qd * 512:(qd + 1) * 512], AF.Copy, [hst[l].regs[qd]], [hstb.regs[qd]])
                for qi in range(nch):
                    cs = slice(qi * Q, (qi + 1) * Q)
                    co = qi * 32
                    for hf in range(2):
                        ptb = (pT[0], pDs[0])[hf]
                        pt = Buf(ptb[:, :] if hf == 0 else ptb[:, :].bitcast(BF16), ptb.regs)
                        for jj in range(8):
                            j = hf * 8 + jj
                            op("pe", lambda h: h.transpose(out=pt[0:Q, jj * 128:(jj + 1) * 128], in_=xc[:, j, cs], identity=identb[:, :]),
                               reads=[xc.regs[j], identb.regs[0]], writes=pt.regs, inc=(jj == 7))
                        pv = pt[0:Q, :].rearrange("p (h d) -> p h d", h=16)
                        ov2 = xdt2[0:Q, hf * 1024:(hf + 1) * 1024].rearrange("p (h d) -> p h d", h=16)
                        act(xtm[0:Q, hf * 1024:(hf + 1) * 1024], pt[0:Q, :], AF.Copy, pt.regs, [xtm.regs[hf]])
                        tt(ov2, pv, bc_last(dte[0:Q, co + hf * 16:co + (hf + 1) * 16], 64), ALU.mult, pt.regs + dr, [xdt2.regs[hf]])
                    pt = Buf(pDs[1][:, :].bitcast(BF16), pDs[1].regs)
                    for g in range(4):
                        op("pe", lambda h: h.transpose(out=pt[0:Q, g * 128:(g + 1) * 128], in_=xc[:, 16 + g, cs], identity=identb[:, :]),
                           reads=[xc.regs[16 + g], identb.regs[0]], writes=pt.regs, inc=(g == 3))
                    act(btm[0:Q, :], pt[0:Q, 0:512], AF.Copy, pt.regs, btm.regs)
                    for g in range(4):
                        op("pe", lambda h: h.matmul(pC[0:Q, g * 128:g * 128 + Q], lhsT=xc[:, 16 + g, cs], rhs=xc[:, 20 + g, cs], start=True, stop=True),
                           reads=[xc.regs[16 + g], xc.regs[20 + g]], writes=pC.regs, inc=(g == 3))
                    for hb in range(1):
                        for h16 in range(32):
                            hh = h16
                            g = hh // 8
                            pDh = (pA[0], pA[1], pB[0], pB[1], pDs[0], pDs[1])[hh % 6]
                            dab = dab16[0:Q, co + hh:co + hh + 1]
                            dlb = dal16[0:Q, co + hh:co + hh + 1]
                            op("pe", lambda h: h.matmul(pDh[0:Q, 0:Q], lhsT=identb[0:Q, 0:Q], rhs=masknegb[0:Q, 0:Q], start=True, stop=False),
                               reads=[identb.regs[0], masknegb.regs[0]], writes=pDh.regs, inc=False)
                            op("pe", lambda h: h.matmul(pDh[0:Q, 0:Q], lhsT=dab.broadcast_to([Q, Q]), rhs=trib[0:Q, 0:Q], start=False, stop=False),
                               reads=dab16.regs + trib.regs, writes=pDh.regs, inc=False)
                            op("pe", lambda h: h.matmul(pDh[0:Q, 0:Q], lhsT=dlb.broadcast_to([Q, Q]), rhs=trib[0:Q, 0:Q], start=False, stop=True),
                               reads=dal16.regs + trib.regs, writes=pDh.regs, inc=True)
                            LT = LTs[hh % 3]
                            act(LT[0:Q, 0:Q], pDh[0:Q, 0:Q], AF.Exp, pDh.regs + dr, LT.regs, bias=nacs[0:Q, co + hh:co + hh + 1])
                            tt(MT[0:Q, h16, 0:Q], pC[0:Q, g * 128:g * 128 + Q], LT[0:Q, 0:Q], ALU.mult, pC.regs + LT.regs, [MT.regs[h16]])
                        for g in range(4):
                            pr = psB()
                            mm_group(pr[0:Q, 0:512], pr.regs[0], [(xc[:, 20 + g, cs], hstb[:, g * 512:(g + 1) * 512], [xc.regs[20 + g], hstb.regs[g]])])
                            tt(yoff[0:Q, g * 512:(g + 1) * 512].rearrange("p (h d) -> p h d", h=8),
                               pr[0:Q, 0:512].rearrange("p (h d) -> p h d", h=8),
                               bc_last(eacs[0:Q, co + g * 8:co + (g + 1) * 8], 64), ALU.mult, pr.regs + eacs.regs, [yoff.regs[g]])
                        for j in range(16):
                            py = psA()
                            for h2 in range(2):
                                hh = 2 * j + h2
                                h16 = hh
                                op("pe", lambda h: h.matmul(py[h2 * 64:(h2 + 1) * 64, 0:Q], lhsT=xtm[0:Q, hh * 64:(hh + 1) * 64], rhs=MT[0:Q, h16, 0:Q], start=True, stop=False),
                                   reads=[xtm.regs[hh // 16], MT.regs[h16]], writes=py.regs, inc=False)
                                op("pe", lambda h: h.matmul(py[h2 * 64:(h2 + 1) * 64, 0:Q], lhsT=yoff[0:Q, hh * 64:(hh + 1) * 64], rhs=identb[0:Q, 0:Q], start=False, stop=True),
                                   reads=[yoff.regs[hh // 8], identb.regs[0]], writes=py.regs, inc=(h2 == 1))
                            yt = tmpf()
                            stt(yt[:, 0:Q], xc[:, j, cs], pc(l, PC_D + j), py[:, 0:Q], ALU.mult, ALU.add, [xc.regs[j]] + pcol.regs + py.regs, yt.regs)
                            tt(sz[:, j, cs], yt[:, 0:Q], sz[:, j, cs], ALU.mult, yt.regs + [sz.regs[j]], [sz.regs[j]])
                    for qd in range(4):
                        pss = psB()
                        for h8 in range(8):
                            hh = qd * 8 + h8
                            op("pe", lambda h: h.matmul(pss[:, h8 * 64:(h8 + 1) * 64], lhsT=btm[0:Q, qd * 128:(qd + 1) * 128], rhs=xdt2[0:Q, hh * 64:(hh + 1) * 64], start=True, stop=True),
                               reads=btm.regs + [xdt2.regs[hh // 16]], writes=pss.regs, inc=(h8 == 7))
                        hv = hst[l][:, qd * 512:(qd + 1) * 512].rearrange("p (h d) -> p h d", h=8)
                        tt(hv, hv, bc_last(cdt[:, co + qd * 8:co + (qd + 1) * 8], 64), ALU.mult, [hst[l].regs[qd]] + cdt.regs, [hst[l].regs[qd]])
                        tt(hst[l][:, qd * 512:(qd + 1) * 512], hst[l][:, qd * 512:(qd + 1) * 512], pss[:, :], ALU.add,
                           [hst[l].regs[qd]] + pss.regs, [hst[l].regs[qd]])
                        if qi < nch - 1:
                            act(hstb[:, qd * 512:(qd + 1) * 512], hst[l][:, qd * 512:(qd + 1) * 512], AF.Copy, [hst[l].regs[qd]], [hstb.regs[qd]])
                esS.close()
                esXC.close()
                rg = k.sb([128, 4, TT], F32, 4, es=esB)
                gbuf = k.sb([128, KC, TT], F32, KC, es=esB)
                nbanks = [pC, pDs[0], pDs[1], Buf(pT[0][:, :].bitcast(F32), pT[0].regs)]
                for g in range(4):
                    bk = nbanks[g]
                    for j4 in range(4):
                        j = g * 4 + j4
                        sq = tmpb()
                        act(sq[:, 0:N], sz[:, j, 0:N], AF.Square, [sz.regs[j]], sq.regs)
                        op("pe", lambda h: h.matmul(bk[:, 0:N], lhsT=onesb[:, :], rhs=sq[:, 0:N], start=(j4 == 0), stop=(j4 == 3)),
                           reads=[onesb.regs[0]] + sq.regs, writes=bk.regs, inc=True)
                for g in range(4):
                    bk = nbanks[g]
                    act(rg[:, g, 0:N], bk[:, 0:N], AF.Ln, bk.regs, [rg.regs[g]], bias=epsc[:, 0:1], scale=1.0 / 512)
                    act(rg[:, g, 0:N], rg[:, g, 0:N], AF.Exp, [rg.regs[g]], [rg.regs[g]], scale=-0.5)
                    for j4 in range(4):
                        j = g * 4 + j4
                        stt(sz[:, j, 0:N], sz[:, j, 0:N], pc(l, PC_SNW + j), rg[:, g, 0:N], ALU.mult, ALU.mult,
                            [sz.regs[j], rg.regs[g]] + pcol.regs, [sz.regs[j]])
                for gi in range(2):
                    gsl = wslot(l, S_G1[gi])
                    for bb in range(4):
                        c = gi * 4 + bb
                        p2 = psB()
                        proj(gsl, bb, hT, N, p2)
                        act(gbuf[:, c, 0:N], p2[:, 0:N], AF.Sigmoid, p2.regs + pcol.regs, [gbuf.regs[c]], bias=pc(l, PC_GB + 8 + c))
                for oi in range(4):
                    so = wslot(l, S_SO[oi])
                    for bb in range(2):
                        c = oi * 2 + bb
                        p1 = psA()
                        proj(so, bb, sz, N, p1, nk=16, bw=256)
                        g_ = tmpf()
                        tt(g_[:, 0:N], p1[:, 0:N], gbuf[:, c, 0:N], ALU.mult, p1.regs + [gbuf.regs[c]], g_.regs)
                        tt(BUFS['m'][:, c, 0:N], BUFS['m'][:, c, 0:N], g_[:, 0:N], ALU.add, [BUFS['m'].regs[c]] + g_.regs, [BUFS['m'].regs[c]])

            stage('B')
            with ExitStack() as esC:
                qT = k.sb([128, KC, TT], BF16, KC, es=esC)
                sxg = k.sb([128, KC, TT], BF16, KC, es=esC)
                Et = [k.sb([128, 2, TT], BF16, es=esC) for _ in range(2)]
                kv = k.sb([128, SLOTW], BF16, es=esC)
                if S.kvi == 0:
                    k.dma("sp", kv[:, :], kvb[0, l], kv_sl, reads=[kvb_reg[0][l]], writes=kv.regs)
                else:
                    cast_into(kv, kvc[S.kvi - 1, l], kv_sl)
                kTv = kv[:, 0:2048].rearrange("p (h c m) -> p h c m", h=4, c=2)
                vv = kv[:, 2048:4096].rearrange("p (c f) -> p c f", c=2)
                for half in range(2):
                    sw = wslot(l, S_Q[half])
                    for b in range(4):
                        c = half * 4 + b
                        p1 = psA()
                        proj(sw, b, hT, N, p1)
                        act(qT[:, c, 0:N], p1[:, 0:N], AF.Copy, p1.regs, [qT.regs[c]])
                for half in range(2):
                    sw = wslot(l, S_XG[half])
                    for b in range(4):
                        c = half * 4 + b
                        p1 = psA()
                        proj(sw, b, hT, N, p1)
                        act(sxg[:, c, 0:N], p1[:, 0:N], AF.Silu, p1.regs, [sxg.regs[c]])
                for hh in range(4):
                    et = Et[hh % 2]
                    for mc in range(2):
                        p1 = psA()
                        mm_group(p1[:, 0:N], p1.regs[0],
                                 [(kTv[:, hh, dc, mc * 128:(mc + 1) * 128], qT[:, hh * 2 + dc, 0:N], [kv.regs[0], qT.regs[hh * 2 + dc]]) for dc in range(2)])
                        act(et[:, mc, 0:N], p1[:, 0:N], AF.Exp, p1.regs, et.regs, scale=1.0 / 16.0)
                    p2 = psB()
                    mm_group(p2[:, 0:N], p2.regs[0], [(onesb[:, :], et[:, mc, 0:N], [onesb.regs[0]] + et.regs) for mc in range(2)])
                    rden = tmpf()
                    act(rden[:, 0:N], p2[:, 0:N], AF.Ln, p2.regs, rden.regs)
                    act(rden[:, 0:N], rden[:, 0:N], AF.Exp, rden.regs, rden.regs, scale=-1.0)
                    for dc in range(2):
                        c = hh * 2 + dc
                        p1 = psA()
                        mm_group(p1[:, 0:N], p1.regs[0],
                                 [(vv[:, mc, hh * 256 + dc * 128:hh * 256 + (dc + 1) * 128], et[:, mc, 0:N], [kv.regs[0]] + et.regs) for mc in range(2)])
                        ot = tmpf()
                        tt(ot[:, 0:N], p1[:, 0:N], rden[:, 0:N], ALU.mult, p1.regs + rden.regs, ot.regs)
                        tt(sxg[:, c, 0:N], ot[:, 0:N], sxg[:, c, 0:N], ALU.mult, ot.regs + [sxg.regs[c]], [sxg.regs[c]])
                gbufc = k.sb([128, KC, TT], F32, KC, es=esC)
                for gi in range(2):
                    gsl = wslot(l, S_G2[gi])
                    for bb in range(4):
                        c = gi * 4 + bb
                        p2 = psB()
                        proj(gsl, bb, hT, N, p2)
                        act(gbufc[:, c, 0:N], p2[:, 0:N], AF.Sigmoid, p2.regs + pcol.regs, [gbufc.regs[c]], bias=pc(l, PC_GB + 16 + c))
                for oi in range(2):
                    so = wslot(l, S_XO[oi])
                    for bb in range(4):
                        c = oi * 4 + bb
                        p1 = psA()
                        proj(so, bb, sxg, N, p1)
                        g_ = tmpf()
                        tt(g_[:, 0:N], p1[:, 0:N], gbufc[:, c, 0:N], ALU.mult, p1.regs + [gbufc.regs[c]], g_.regs)
                        tt(BUFS['m'][:, c, 0:N], BUFS['m'][:, c, 0:N], g_[:, 0:N], ALU.add, [BUFS['m'].regs[c]] + g_.regs, [BUFS['m'].regs[c]])
                k.bury([qT, sxg, kv] + Et)

            stage('C')
            with ExitStack() as esO:
                mb = k.sb([128, KC, TT], BF16, KC, es=esO)
                o32 = k.sb([128, KC, TT], F32, KC, es=esO)
                for c in range(KC):
                    act(mb[:, c, 0:N], BUFS['m'][:, c, 0:N], AF.Copy, [BUFS['m'].regs[c]], [mb.regs[c]])
                if getattr(S, 'prefetch', None):
                    S.prefetch()
                    S.prefetch = None
                for half in range(2):
                    sw = wslot(l, S_WO[half])
                    for b in range(4):
                        c = half * 4 + b
                        p1 = psA()
                        proj(sw, b, mb, N, p1)
                        sq = tmpb()
                        act(sq[:, 0:N], p1[:, 0:N], AF.Square, p1.regs, sq.regs)
                        op("dve", lambda h: h.tensor_copy(out=o32[:, c, 0:N], in_=p1[:, 0:N]), reads=p1.regs, writes=[o32.regs[c]])
                        op("pe", lambda h: h.matmul(pC[:, 0:N], lhsT=onesb[:, :], rhs=sq[:, 0:N], start=(c == 0), stop=(c == KC - 1)),
                           reads=[onesb.regs[0]] + sq.regs, writes=pC.regs, inc=True)
                rr = tmpf()
                rstd_from(pC, N, 1.0 / D, rr)
                if getattr(S, 'next_prenorm', None):
                    S.next_prenorm()
                    S.next_prenorm = None
                for c in range(KC):
                    stt(o32[:, c, 0:N], o32[:, c, 0:N], pc(l, PC_NPOST + c), rr[:, 0:N], ALU.mult, ALU.mult,
                        [o32.regs[c]] + rr.regs + pcol.regs, [o32.regs[c]])
                    tt(BUFS['x'][:, c, 0:N], BUFS['x'][:, c, 0:N], o32[:, c, 0:N], ALU.add, [BUFS['x'].regs[c], o32.regs[c]], [BUFS['x'].regs[c]])
                k.bury([mb, o32])

        def merge_branch(l, bi, N, s_out, s_g, src, nk, bw):
            nblk = bw // 128
            ci = 0
            gs = [None, None]
            order = []
            per_g = 4 // nblk
            for gi in range(2):
                outs = []
                for oi in range(per_g):
                    outs.append(wslot(l, s_out[gi * per_g + oi]))
                gsl = wslot(l, s_g[gi])
                for b in range(4):
                    c = gi * 4 + b
                    so = outs[b // nblk]
                    p1 = psA(); p2 = psB()
                    proj(so, b % nblk, src, N, p1, nk=nk, bw=bw)
                    proj(gsl, b, hT, N, p2)
                    g_ = tmpf()
                    act(g_[:, 0:N], p2[:, 0:N], AF.Sigmoid, p2.regs + pcol.regs, g_.regs, bias=pc(l, PC_GB + bi * 8 + c))
                    if bi == 0:
                        tt(BUFS['m'][:, c, 0:N], p1[:, 0:N], g_[:, 0:N], ALU.mult, p1.regs + g_.regs, [BUFS['m'].regs[c]])
                    else:
                        tt(g_[:, 0:N], p1[:, 0:N], g_[:, 0:N], ALU.mult, p1.regs + g_.regs, g_.regs)
                        tt(BUFS['m'][:, c, 0:N], BUFS['m'][:, c, 0:N], g_[:, 0:N], ALU.add, [BUFS['m'].regs[c]] + g_.regs, [BUFS['m'].regs[c]])

        def run_seq(kind, si):
            S = Seq()
            S.kvi = 0 if kind == "p" else 1 + si
            if kind == "p":
                Tn, N, Q = T_PROMPT, TT_P, Q_P
            else:
                Tn, N, Q = T_S, T_S, T_S
            ntile = Tn // N
            if kind == "p":
                for l in range(2):
                    op("dve", lambda h: h.memset(hst[l][:, :], 0.0), writes=hst[l].regs)
                    op("dve", lambda h: h.memset(halo_a[l][:, :, :], 0.0), writes=halo_a[l].regs)
                    op("dve", lambda h: h.memset(halo_s[l][:, :, :], 0.0), writes=halo_s[l].regs)
                with ExitStack() as esK:
                    memx = k.sb([128, KC, 256], F32, es=esK)
                    memq = k.sb([128, KC, 256], BF16, es=esK)
                    memn = k.sb([128, KC, 256], BF16, KC, es=esK)
                    kv32 = k.sb([128, SLOTW], F32, es=esK)
                    kvbf = k.sb([128, SLOTW], BF16, es=esK)
                    k.dma("pool", memx[:, :, :], mem_d, ioslot(), writes=memx.regs)
                    stage('kv0')
                    act(memq[:, :, :], memx[:, :, :], AF.Square, memx.regs, memq.regs)
                    mm_group(pC[:, 0:256], pC.regs[0], [(onesb[:, :], memq[:, kc, :], [onesb.regs[0]] + memq.regs) for kc in range(KC)])
                    stage('kv1')
                    rm = tmpf()
                    rstd_from(pC, 256, 1.0 / D, rm)
                    stage('kv2')
                    for l in range(2):
                        for kc in range(KC):
                            stt(memn[:, kc, :], memx[:, kc, :], pc(l, PC_MNW + kc), rm[:, 0:256], ALU.mult, ALU.mult,
                                memx.regs + rm.regs + pcol.regs, [memn.regs[kc]])
                        stage('kv3')
                        for half in range(2):
                            sw = wslot(l, S_KVK[half])
                            stage('kv3a')
                            for b in range(4):
                                blk = half * 4 + b
                                p1 = psA()
                                proj(sw, b, memn, 256, p1)
                                stage('kv3b')
                                act(kvbf[:, blk * 256:(blk + 1) * 256], p1[:, 0:256], AF.Copy, p1.regs, kvbf.regs)
                                stage('kv3b2')
                                op("dve", lambda h: h.tensor_copy(out=kv32[:, blk * 256:(blk + 1) * 256], in_=p1[:, 0:256]), reads=p1.regs, writes=kv32.regs)
                                stage('kv3b3')
                        stage('kv3c')
                        for half in range(2):
                            sw = wslot(l, S_KVV[half])
                            sv = sw[:, :].rearrange("p (k c) -> p k c", k=KC)
                            for mc in range(2):
                                p1 = psA()
                                mm_group(p1[:, 0:512], p1.regs[0],
                                         [(memn[:, kc, mc * 128:(mc + 1) * 128], sv[:, kc, :], [memn.regs[kc], sw.regs[0]]) for kc in range(KC)])
                                o0 = 2048 + mc * 1024 + half * 512
                                act(kvbf[:, o0:o0 + 512], p1[:, 0:512], AF.Copy, p1.regs, kvbf.regs)
                                op("dve", lambda h: h.tensor_copy(out=kv32[:, o0:o0 + 512], in_=p1[:, 0:512]), reads=p1.regs, writes=kv32.regs)
                        stage('kv4')
                        k.dma("pool", kvb[0, l], kvbf[:, :], ioslot(), reads=kvbf.regs, writes=[kvb_reg[0][l]])
                        k.dma("pool", mk_o[l], kv32[:, 0:2048], ioslot(), reads=kv32.regs)
                        k.dma("pool", mv_o[l], kv32[:, 2048:4096], ioslot(), reads=kv32.regs)
                    k.bury([memx, memq, memn, kv32, kvbf])
            else:
                with ExitStack() as esK:
                    ha = k.sb([128, KC, 30], F32, es=esK)
                    hs = k.sb([128, 24, 3], F32, es=esK)
                    for l in range(2):
                        k.dma("pool", hst[l][:, :], ssm_d[si, l], ioslot(), writes=hst[l].regs)
                        k.dma("pool", ha[:, :, :], sca_d[si, l], ioslot(), writes=ha.regs)
                        k.dma("pool", hs[:, :, :], scs_d[si, l], ioslot(), writes=hs.regs)
                        op("act", lambda h: h.activation(out=halo_a[l][:, :, :], in_=ha[:, :, :], func=AF.Copy), reads=ha.regs, writes=halo_a[l].regs)
                        op("act", lambda h: h.activation(out=halo_s[l][:, :, :], in_=hs[:, :, :], func=AF.Copy), reads=hs.regs, writes=halo_s[l].regs)
                    k.bury([ha, hs])
            stage('kv')
            for ti in range(ntile):
                if kind == "p":
                    src = xp_d[:, :, ti * N:(ti + 1) * N]
                    dst = yp_o[:, :, ti * N:(ti + 1) * N]
                else:
                    src = xs_d[si]
                    dst = ys_o[si]
                if not GS['preloaded']:
                    k.dma("pool", BUFS['x'][:, :, 0:N], src, ioslot(), writes=BUFS['x'].regs)
                GS['preloaded'] = False
                if GS['prenorm_done']:
                    S.prenorm_done = True
                    GS['prenorm_done'] = False
                if kind == "p" and ti + 1 < ntile:
                    nxt = (xp_d[:, :, (ti + 1) * N:(ti + 2) * N], N)
                elif kind == "p" and N_SAMP > 0:
                    nxt = (xs_d[0], T_S)
                elif kind == "s" and si + 1 < N_SAMP:
                    nxt = (xs_d[si + 1], T_S)
                else:
                    nxt = None
                for l in range(2):
                    if l == 1 and nxt is not None:
                        def _pf(nxt=nxt):
                            k.dma("pool", BUFS['m'][:, :, 0:nxt[1]], nxt[0], ioslot(), writes=BUFS['m'].regs)
                            GS['preloaded'] = True
                        S.prefetch = _pf

                        def _pn(nxt=nxt):
                            prenorm(0, nxt[1], BUFS['m'])
                            GS['prenorm_done'] = True
                        S.next_prenorm = _pn
                    with ExitStack() as esl:
                        S.es_l = esl
                        layer_tile(S, l, N, Q, ti == 0, ti == ntile - 1)
                        LT_COUNT[0] += 1
                        stage('L%d' % LT_COUNT[0])
                k.dma("pool", dst, BUFS['x'][:, :, 0:N], ioslot(), reads=BUFS['x'].regs)
                if GS['preloaded']:
                    BUFS['x'], BUFS['m'] = BUFS['m'], BUFS['x']
            for l in range(2):
                if kind == "p":
                    oa, os_, oh = cap_o[l], csp_o[l], hp_o[l]
                else:
                    oa, os_, oh = cas_o[si, l], css_o[si, l], hs_o[si, l]
                k.dma("pool", oa, ulast[l][:, :, :], ioslot(), reads=ulast[l].regs)
                k.dma("pool", os_, xlast[l][:, :, :], ioslot(), reads=xlast[l].regs)
                k.dma("pool", oh, hst[l][:, :], ioslot(), reads=hst[l].regs)

        try:
            stage('prepass')
            run_seq("p", 0)
            for si in range(N_SAMP):
                run_seq("s", si)
        except _Stop as e_:
            print('STOPPED at', e_)
        k.finish()
        print("instructions:", k.ninst, "waits:", k.nwait, "arena peak", k.apeak)
    return nc


def _fm(x):
    T, C = x.shape
    return np.ascontiguousarray(x.T.reshape(C // 128, 128, T).transpose(1, 0, 2))


def _fm_inv(a):
    p, kc, T = a.shape
    return np.ascontiguousarray(a.transpose(1, 0, 2).reshape(kc * p, T).T)


def _slot_proj(W):
    K, N = W.shape
    return np.ascontiguousarray(W.reshape(K // 128, 128, N).transpose(1, 0, 2)).reshape(128, -1)


def _col(v):
    return np.ascontiguousarray(v.reshape(-1, 128).T)


def prep_weights(inp):
    f = np.float32
    WS = np.zeros((2, NS, 128, SLOTW), f)
    wdt = np.zeros((2, 128, KC * 32), f)
    pcol = np.zeros((128, 256), f)
    prow = np.zeros((128, 128), f)
    o = [0, 1024, 2048, 3072, 5120, 8192, 8224, 9248, 10272, 13344]
    for l in range(2):
        w = np.asarray(inp["w_in"][l])

        def sec(off, i):
            return _slot_proj(w[:, off + i * 512: off + (i + 1) * 512])
        for i in range(2):
            WS[l, S_GV[i]] = sec(o[0], i)
            WS[l, S_GG[i]] = sec(o[1], i)
            WS[l, S_CG[i]] = sec(o[2], i)
            WS[l, S_Q[i]] = sec(o[6], i)
            WS[l, S_XG[i]] = sec(o[7], i)
            WS[l, S_G0[i]] = sec(o[8], i)
            WS[l, S_G1[i]] = sec(o[8] + 1024, i)
            WS[l, S_G2[i]] = sec(o[8] + 2048, i)
            WS[l, S_CO[i]] = _slot_proj(np.asarray(inp["conv_out_w"][l])[:, i * 512:(i + 1) * 512])
            WS[l, S_XO[i]] = _slot_proj(np.asarray(inp["xa_out_w"][l])[:, i * 512:(i + 1) * 512])
            WS[l, S_WO[i]] = _slot_proj(np.asarray(inp["w_out"][l])[:, i * 512:(i + 1) * 512])
            WS[l, S_KVK[i]] = _slot_proj(np.asarray(inp["xa_kv_w"][l])[:, i * 512:(i + 1) * 512])
            WS[l, S_KVV[i]] = _slot_proj(np.asarray(inp["xa_kv_w"][l])[:, 1024 + i * 512:1024 + (i + 1) * 512])
        for i in range(4):
            WS[l, S_Z + i] = sec(o[3], i)
            WS[l, S_SO[i]] = _slot_proj(np.asarray(inp["ssd_out_w"][l])[:, i * 256:(i + 1) * 256])
        for gi in range(3):
            for hf in range(2):
                WS[l, S_XBC[gi][hf]] = sec(o[4], gi * 2 + hf)
        wdt[l] = _slot_proj(w[:, o[5]:o[5] + 32])
        cw = np.asarray(inp["conv_dw_w"][l])
        idx = np.arange(128)
        for c in range(KC):
            a = np.zeros((128, 32, 128), f)
            a[idx, :31, idx] = cw[:, c * 128:(c + 1) * 128].T
            WS[l, S_DA + c] = a.reshape(128, -1)
        sw = np.asarray(inp["ssd_conv_w"][l])
        for gi in range(3):
            a = np.zeros((128, 8, 4, 128), f)
            for jj in range(8):
                j = gi * 8 + jj
                a[idx, jj, :, idx] = sw[:, j * 128:(j + 1) * 128].T
            WS[l, S_DS[gi]] = a.reshape(128, -1)
        b = l * 128
        pcol[:, b + PC_NPRE:b + PC_NPRE + 8] = _col(np.asarray(inp["norm_pre_w"][l]))
        pcol[:, b + PC_GB:b + PC_GB + 24] = _col(np.asarray(inp["gate_b"][l]))
        pcol[:, b + PC_DWB:b + PC_DWB + 8] = _col(np.asarray(inp["conv_dw_b"][l]))
        pcol[:, b + PC_LNW:b + PC_LNW + 8] = _col(np.asarray(inp["conv_ln_w"][l]))
        pcol[:, b + PC_LNB:b + PC_LNB + 8] = _col(np.asarray(inp["conv_ln_b"][l]))
        pcol[:, b + PC_SCB:b + PC_SCB + 24] = _col(np.asarray(inp["ssd_conv_b"][l]))
        pcol[:, b + PC_D:b + PC_D + 16] = _col(np.repeat(np.asarray(inp["ssd_d"][l]), 64))
        pcol[:, b + PC_SNW:b + PC_SNW + 16] = _col(np.asarray(inp["ssd_norm_w"][l]))
        pcol[:, b + PC_NPOST:b + PC_NPOST + 8] = _col(np.asarray(inp["norm_post_w"][l]))
        pcol[:, b + PC_MNW:b + PC_MNW + 8] = _col(np.asarray(inp["mem_norm_w"][l]))
        prow[:, l * 64:l * 64 + 32] = np.asarray(inp["ssd_dt_bias"][l])[None, :]
        prow[:, l * 64 + 32:l * 64 + 64] = np.asarray(inp["ssd_a_log"][l])[None, :]
    cst = np.zeros((128, 512), f)
    i = np.arange(128)
    cst[:, 0:128] = np.eye(128, dtype=f)
    cst[:, 128:256] = (i[:, None] <= i[None, :]).astype(f)
    cst[:, 256:384] = np.where(i[None, :] < i[:, None], -30000.0, 0.0).astype(f)
    cst[:, 384:512] = 1.0
    return dict(ws=WS, wdt=wdt, pcol=pcol, prow=prow, cst=cst)


def core_inputs(inp, shared, c, n_samp=2):
    f = np.float32
    m = dict(shared)
    m["xp"] = _fm(np.asarray(inp["x_prompt"][c], f))
    sidx = [c * n_samp + i for i in range(n_samp)]
    m["xs"] = np.stack([_fm(np.asarray(inp["x_sample"][s], f)) for s in sidx])
    m["memT"] = _fm(np.asarray(inp["mem_prompt"][c], f))
    kvc = np.zeros((n_samp, 2, 128, SLOTW), f)
    sca = np.zeros((n_samp, 2, 128, KC, 30), f)
    scs = np.zeros((n_samp, 2, 128, 24, 3), f)
    ssm = np.zeros((n_samp, 2, 128, 2048), f)
    for i, s in enumerate(sidx):
        for l in range(2):
            kk = np.asarray(inp["cache_mem_k"][l, s], f)
            kvc[i, l, :, 0:2048] = kk.reshape(256, 4, 2, 128).transpose(3, 1, 2, 0).reshape(128, 2048)
            vv = np.asarray(inp["cache_mem_v"][l, s], f).reshape(256, 1024)
            kvc[i, l, :, 2048:4096] = vv.reshape(2, 128, 1024).transpose(1, 0, 2).reshape(128, 2048)
            sca[i, l] = _fm(np.asarray(inp["state_conv_a"][l, s], f))
            scs[i, l] = _fm(np.asarray(inp["state_conv_ssd"][l, s], f))
            ssm[i, l] = np.asarray(inp["state_ssm"][l, s], f).reshape(2048, 128).T
    m.update(kvc=kvc, sca=sca, scs=scs, ssm=ssm)
    return m


def assemble(results, n_cores, T, n_samp=2, t_s=64):
    f = np.float32
    B = n_cores
    yp = np.zeros((B, T, D), f); ys = np.zeros((B * n_samp, t_s, D), f)
    mk = np.zeros((2, B, 256, 4, 256), f); mv = np.zeros((2, B, 256, 4, 256), f)
    cap = np.zeros((2, B, 30, 1024), f); csp = np.zeros((2, B, 3, 3072), f); hp = np.zeros((2, B, 32, 64, 128), f)
    cas = np.zeros((2, B * n_samp, 30, 1024), f); css = np.zeros((2, B * n_samp, 3, 3072), f)
    hs = np.zeros((2, B * n_samp, 32, 64, 128), f)
    for c, r in enumerate(results):
        yp[c] = _fm_inv(r["yp"])
        for l in range(2):
            mk[l, c] = r["mk"][l].reshape(128, 4, 2, 256).transpose(3, 1, 2, 0).reshape(256, 4, 256)
            mv[l, c] = r["mv"][l].reshape(128, 2, 1024).transpose(1, 0, 2).reshape(256, 4, 256)
            cap[l, c] = _fm_inv(r["cap"][l])
            csp[l, c] = _fm_inv(r["csp"][l])
            hp[l, c] = r["hp"][l].T.reshape(32, 64, 128)
        for i in range(n_samp):
            s = c * n_samp + i
            ys[s] = _fm_inv(r["ys"][i])
            for l in range(2):
                cas[l, s] = _fm_inv(r["cas"][i, l])
                css[l, s] = _fm_inv(r["css"][i, l])
                hs[l, s] = r["hs"][i, l].T.reshape(32, 64, 128)
    return (yp, ys, mk, mv, cap, csp, hp, cas, css, hs)


def kernel(**inp):
    n = 8
    T = np.asarray(inp["x_prompt"]).shape[1]
    nc = build(T_PROMPT=T)
    shared = prep_weights(inp)
    in_maps = [core_inputs(inp, shared, c) for c in range(n)]
    res = run_bass_kernel_spmd(nc, in_maps, core_ids=list(range(n)))
    return assemble(res.results, n, T)
```

```python
import numpy as np
from contextlib import ExitStack
import concourse.bass as bass
import concourse.mybir as mybir
from concourse.bass_utils import run_bass_kernel_spmd

F32 = mybir.dt.float32
BF16 = mybir.dt.bfloat16
AF = mybir.ActivationFunctionType
ALU = mybir.AluOpType

D = 1024
KC = 8
NS = 51
SLOTW = 4096
EPS = 1e-6
import os
SAME_ENG_WAIT = os.environ.get("SAME_ENG_WAIT", "1") == "1"
DBG_STOP = os.environ.get('DBG_STOP', '')


class _Stop(Exception):
    pass


STOPPED = [False]
LT_COUNT = [0]


PHASES = []
PE_OPS = [0]


def stage(name):
    PHASES.append((LT_COUNT[0], name, PE_OPS[0]))
    if DBG_STOP == name and not STOPPED[0]:
        STOPPED[0] = True
        print('STOPPED at', name)

S_GV = (0, 2); S_GG = (1, 3)
S_DA = 4
S_CG = (12, 13)
S_G0 = (14, 15); S_CO = (16, 17)
S_Z = 18
S_XBC = ((22, 23), (25, 26), (28, 29)); S_DS = (24, 27, 30)
S_G1 = (31, 32); S_SO = (33, 34, 35, 36)
S_Q = (37, 38); S_XG = (39, 40)
S_G2 = (41, 42); S_XO = (43, 44)
S_WO = (45, 46)
S_KVK = (47, 48); S_KVV = (49, 50)
NS_MAIN = 47

PC_NPRE = 0; PC_GB = 8; PC_DWB = 32; PC_LNW = 40; PC_LNB = 48; PC_SCB = 56
PC_D = 80; PC_SNW = 96; PC_NPOST = 112; PC_MNW = 120


class Reg:
    __slots__ = ("w", "r", "excl")

    def __init__(self, init=None, excl=False):
        self.w = dict(init) if init else {}
        self.r = {}
        self.excl = excl


class Eng:
    def __init__(self, name, h, sem):
        self.name = name; self.h = h; self.sem = sem; self.cnt = 0
        self.key = "e_" + name
        self.seen = {}


class DSlot:
    def __init__(self, key, sem):
        self.key = key; self.sem = sem; self.cnt = 0


class Buf:
    def __init__(self, t, regs):
        self.t = t; self.regs = regs

    def __getitem__(self, idx):
        return self.t[idx]


class KB:
    def __init__(self, nc, es):
        self.nc = nc; self.es = es
        self.E = {}
        for name, h in (("pe", nc.tensor), ("act", nc.scalar), ("dve", nc.vector),
                        ("pool", nc.gpsimd), ("sp", nc.sync)):
            sem = es.enter_context(nc.semaphore("sem_" + name))
            self.E[name] = Eng(name, h, sem)
        self.grave = {}
        self.uid = 0
        self.dslots = []
        self.nwait = 0
        self.ninst = 0

    def name(self, p):
        self.uid += 1
        return "%s%d" % (p, self.uid)

    def dslot(self):
        sem = self.es.enter_context(self.nc.semaphore(self.name("dsem")))
        s = DSlot(self.name("d"), sem)
        self.dslots.append(s)
        return s

    def init_arena(self, nbytes):
        self.ARENA = nbytes
        self.arena = self.es.enter_context(self.nc.sbuf_tensor("arena", [128, nbytes // 2], BF16))
        self.ablk = [Reg() for _ in range(nbytes // 1024 + 2)]
        self.atop = 0
        self.apeak = 0

    def sb(self, shape, dt, nreg=1, es=None, own_block=False):
        if es is not None:
            esz = 4 if dt == F32 else 2
            per = 1
            for d_ in shape[1:]:
                per *= d_
            nb = per * esz
            off = (self.atop + 1023) // 1024 * 1024 if (nb >= 1024 or own_block) else (self.atop + 63) // 64 * 64
            assert off + nb <= self.ARENA, ("arena overflow", off, nb, self.ARENA)
            prev = self.atop
            self.atop = off + (max(nb, 1024) if own_block else nb)
            self.apeak = max(self.apeak, self.atop)

            def _rel(prev=prev):
                self.atop = prev
            es.callback(_rel)
            ap = self.arena[:, off // 2:(off + nb) // 2]
            if dt == F32:
                ap = ap.bitcast(F32)
            if len(shape) == 3:
                ap = ap.rearrange("p (a b) -> p a b", a=shape[1])
            elif len(shape) == 4:
                ap = ap.rearrange("p (a b c) -> p a b c", a=shape[1], b=shape[2])
            regs = []
            for i in range(nreg):
                b0 = off + i * nb // nreg
                b1 = off + (i + 1) * nb // nreg
                regs.append([self.ablk[j] for j in range(b0 // 1024, (b1 - 1) // 1024 + 1)])
            return Buf(ap, regs)
        nm = self.name("sb")
        if os.environ.get('DBG_ALLOC'):
            print('alloc', nm, shape, dt, 'scoped' if es is not None else 'persist')
        t = (es if es is not None else self.es).enter_context(self.nc.sbuf_tensor(nm, list(shape), dt))
        return Buf(t, [Reg(self.grave) for _ in range(nreg)])

    def psum(self, shape, dt, nreg=1):
        t = self.es.enter_context(self.nc.psum_tensor(self.name("ps"), list(shape), dt))
        return Buf(t, [Reg(excl=True) for _ in range(nreg)])

    def bury(self, bufs):
        return
        for b in bufs:
            for r in b.regs:
                for d in (r.w, r.r):
                    for k, v in d.items():
                        if k not in self.grave or self.grave[k][1] < v[1]:
                            self.grave[k] = v

    @staticmethod
    def flat(regs):
        out = []
        for r in regs:
            if isinstance(r, (list, tuple)):
                out.extend(KB.flat(r))
            else:
                out.append(r)
        return out

    def _waits(self, e, reads, writes):
        deps = {}
        for r in reads:
            for k, v in r.w.items():
                if k not in deps or deps[k][1] < v[1]:
                    deps[k] = v
            if r.excl:
                for k, v in r.r.items():
                    if k != e.key and (k not in deps or deps[k][1] < v[1]):
                        deps[k] = v
        for w in writes:
            for d in (w.w, w.r):
                for k, v in d.items():
                    if k not in deps or deps[k][1] < v[1]:
                        deps[k] = v
        for k, (sem, cnt) in deps.items():
            if k == e.key and (e.name == "pe" or not SAME_ENG_WAIT):
                continue
            if e.seen.get(k, 0) >= cnt:
                continue
            e.h.wait_ge(sem, cnt)
            e.seen[k] = cnt
            self.nwait += 1
            if os.environ.get('DBG_WAITS'):
                print('  wait', e.name, 'on', k, cnt)

    def _mark(self, ev, reads, writes):
        k, sem, cnt = ev
        for r in reads:
            if k not in r.r or r.r[k][1] < cnt:
                r.r[k] = (sem, cnt)
        for w in writes:
            w.w = {k: (sem, cnt)}
            w.r = {}

    def op(self, eng, fn, reads=(), writes=(), inc=True):
        if STOPPED[0]:
            return
        e = self.E[eng]
        reads = self.flat(reads); writes = self.flat(writes)
        self._waits(e, reads, writes)
        ins = fn(e.h)
        self.ninst += 1
        if eng == 'pe':
            PE_OPS[0] += 1
        if os.environ.get('DBG_WAITS'):
            print('op', eng, self.ninst, ('inc->%d' % (e.cnt + 1)) if inc else '')
        if inc:
            e.cnt += 1
            ins.then_inc(e.sem, 1)
            ev = (e.key, e.sem, e.cnt)
        else:
            ev = (e.key, e.sem, e.cnt + 1)
        self._mark(ev, reads, writes)

    def dma(self, q, out, in_, slot, reads=(), writes=(), **kw):
        if STOPPED[0]:
            return
        e = self.E[q]
        reads = self.flat(reads); writes = self.flat(writes)
        self._waits(e, reads, writes)
        if slot.cnt > 0 and e.seen.get(slot.key, 0) < slot.cnt:
            e.h.wait_ge(slot.sem, slot.cnt)
            e.seen[slot.key] = slot.cnt
        ins = e.h.dma_start(out=out, in_=in_, **kw)
        slot.cnt += 16
        ins.then_inc(slot.sem, 16)
        self.ninst += 1
        self._mark((slot.key, slot.sem, slot.cnt), reads, writes)

    def finish(self):
        e = self.E["sp"]
        for s in self.dslots:
            if s.cnt > 0 and e.seen.get(s.key, 0) < s.cnt:
                e.h.wait_ge(s.sem, s.cnt)
                e.seen[s.key] = s.cnt


def bc_last(ap, n):
    sh = list(ap.shape)
    return ap.unsqueeze(len(sh)).broadcast_to(sh + [n])


class Seq:
    pass


def build(T_PROMPT=4096, TT_P=512, Q_P=128, N_SAMP=2, T_S=64):
    STOPPED[0] = False
    LT_COUNT[0] = 0
    nc = bass.Bass("TRN2", target_bir_lowering=False)
    NT_P = T_PROMPT // TT_P
    def din(name, shape, dt=F32):
        return nc.dram_tensor(name, list(shape), dt, kind="ExternalInput").ap()

    def dout(name, shape, dt=F32):
        return nc.dram_tensor(name, list(shape), dt, kind="ExternalOutput").ap()

    ws = din("ws", [2, NS, 128, SLOTW])
    kvc = din("kvc", [N_SAMP, 2, 128, SLOTW])
    wdt_d = din("wdt", [2, 128, KC * 32])
    pcol_d = din("pcol", [128, 256])
    prow_d = din("prow", [128, 128])
    cst_d = din("cst", [128, 512])
    xp_d = din("xp", [128, KC, T_PROMPT])
    xs_d = din("xs", [N_SAMP, 128, KC, T_S])
    mem_d = din("memT", [128, KC, 256])
    sca_d = din("sca", [N_SAMP, 2, 128, KC, 30])
    scs_d = din("scs", [N_SAMP, 2, 128, 24, 3])
    ssm_d = din("ssm", [N_SAMP, 2, 128, 2048])

    yp_o = dout("yp", [128, KC, T_PROMPT])
    ys_o = dout("ys", [N_SAMP, 128, KC, T_S])
    mk_o = dout("mk", [2, 128, 2048])
    mv_o = dout("mv", [2, 128, 2048])
    cap_o = dout("cap", [2, 128, KC, 30])
    csp_o = dout("csp", [2, 128, 24, 3])
    hp_o = dout("hp", [2, 128, 2048])
    cas_o = dout("cas", [N_SAMP, 2, 128, KC, 30])
    css_o = dout("css", [N_SAMP, 2, 128, 24, 3])
    hs_o = dout("hs", [N_SAMP, 2, 128, 2048])

    wsb = nc.dram_tensor("wsb", [2, NS, 128, SLOTW], BF16).ap()
    kvb = nc.dram_tensor("kvb", [1 + N_SAMP, 2, 128, SLOTW], BF16).ap()
    wsb_reg = [[Reg() for _ in range(NS)] for _ in range(2)]
    kvb_reg = [[Reg() for _ in range(2)] for _ in range(1 + N_SAMP)]

    with ExitStack() as es:
        k = KB(nc, es)
        k.init_arena(int(os.environ.get('ARENA_KB', '72')) * 1024)
        op = k.op

        cstf = k.sb([128, 512], F32)
        identb = k.sb([128, 128], BF16)
        masknegb = k.sb([128, 128], BF16)
        onesb = k.sb([128, 128], BF16)
        trib = k.sb([128, 128], BF16)
        pcol = k.sb([128, 256], F32)
        prow = k.sb([128, 128], F32)
        arow = k.sb([128, 64], F32)
        es0 = ExitStack()
        wdtf = k.sb([128, 2 * KC * 32], F32, es=es0)
        wdtb = k.sb([128, 2, KC, 32], BF16)
        ld0 = k.dslot()
        k.dma("pool", cstf[:, :], cst_d, ld0, writes=cstf.regs)
        k.dma("pool", pcol[:, :], pcol_d, ld0, writes=pcol.regs)
        k.dma("pool", prow[:, :], prow_d, ld0, writes=prow.regs)
        k.dma("pool", wdtf[:, 0:256], wdt_d[0], ld0, writes=wdtf.regs)
        k.dma("pool", wdtf[:, 256:512], wdt_d[1], ld0, writes=wdtf.regs)
        identf = cstf[:, 0:128]
        trif = cstf[:, 128:256]
        onesf = cstf[:, 384:512]
        op("act", lambda h: h.activation(out=identb[:, :], in_=cstf[:, 0:128], func=AF.Copy),
           reads=cstf.regs, writes=identb.regs)
        op("act", lambda h: h.activation(out=masknegb[:, :], in_=cstf[:, 256:384], func=AF.Copy),
           reads=cstf.regs, writes=masknegb.regs)
        op("act", lambda h: h.activation(out=onesb[:, :], in_=cstf[:, 384:512], func=AF.Copy),
           reads=cstf.regs, writes=onesb.regs)
        op("act", lambda h: h.activation(out=trib[:, :], in_=cstf[:, 128:256], func=AF.Copy),
           reads=cstf.regs, writes=trib.regs)
        op("act", lambda h: h.activation(out=wdtb[:, :, :, :].rearrange("p a b c -> p (a b c)"), in_=wdtf[:, :], func=AF.Copy),
           reads=wdtf.regs, writes=wdtb.regs)
        es0.close()
        for l in range(2):
            op("act", lambda h: h.activation(out=arow[:, l * 32:(l + 1) * 32], in_=prow[:, l * 64 + 32:l * 64 + 64], func=AF.Exp),
               reads=prow.regs, writes=arow.regs)
        op("dve", lambda h: h.tensor_scalar(out=arow[:, :], in0=arow[:, :], scalar1=-1.0, scalar2=None, op0=ALU.mult),
           reads=arow.regs, writes=arow.regs)

        TT = TT_P
        NRING = 7
        LOOKAHEAD = 3
        ring = [k.sb([128, SLOTW], BF16) for _ in range(NRING)]
        ring_sl = [k.dslot() for _ in range(NRING)]
        ring_i = [0]

        def load_slot(src_ap, src_reg):
            i = ring_i[0] % NRING
            ring_i[0] += 1
            k.dma("sp", ring[i][:, :], src_ap, ring_sl[i], reads=[src_reg], writes=ring[i].regs)
            return ring[i]

        cast_done = [[False] * NS for _ in range(2)]
        sto_sl = [k.dslot() for _ in range(2)]

        def cast_into(dst, src_ap, slot):
            k.dma("pool", dst[:, :], src_ap, slot, writes=dst.regs, max_dma_last_dim=8192)

        prepared = {}
        FINAL = [False]

        def prepare(l, s):
            i = ring_i[0] % NRING
            ring_i[0] += 1
            r = ring[i]
            if cast_done[l][s]:
                if os.environ.get('DBG_WAITS'):
                    print('RINGLOAD', l, s, 'ring', i)
                k.dma("sp", r[:, :], wsb[l, s], ring_sl[i], reads=[wsb_reg[l][s]], writes=r.regs)
                return r
            cast_done[l][s] = True
            cast_into(r, ws[l, s], ring_sl[i])
            if s < NS_MAIN:
                k.dma("sp", wsb[l, s], r[:, :], sto_sl[s % 2], reads=r.regs, writes=[wsb_reg[l][s]])
            return r

        def wslot(l, s):
            r = prepared.pop((l, s), None)
            if r is None:
                r = prepare(l, s)
            if s < NS_MAIN:
                nxt = (l, s)
                for _ in range(LOOKAHEAD):
                    nxt = (nxt[0], nxt[1] + 1) if nxt[1] + 1 < NS_MAIN else (1 - nxt[0], 0)
                    if FINAL[0] and nxt[0] == 0:
                        break
                    if nxt not in prepared:
                        prepared[nxt] = prepare(*nxt)
            return r

        bufA = k.sb([128, KC, TT], F32, KC)
        hT = k.sb([128, KC, TT], BF16, KC)
        bufB = k.sb([128, KC, TT], F32, KC)
        BUFS = {'x': bufA, 'm': bufB}
        GS = {'preloaded': False, 'prenorm_done': False}
        hst = [k.sb([128, 2048], F32, 4) for _ in range(2)]
        hstb = k.sb([128, 2048], BF16, 4)
        halo_a = [k.sb([128, KC, 30], BF16) for _ in range(2)]
        halo_s = [k.sb([128, 24, 3], BF16) for _ in range(2)]
        ulast = [k.sb([128, KC, 30], F32) for _ in range(2)]
        xlast = [k.sb([128, 24, 3], F32) for _ in range(2)]
        kv_sl = k.dslot()
        io_sl = [k.dslot() for _ in range(4)]
        io_i = [0]

        def ioslot():
            io_i[0] += 1
            return io_sl[io_i[0] % 4]

        pA = [k.psum([128, 512], F32) for _ in range(2)]
        pB = [k.psum([128, 512], F32) for _ in range(2)]
        pC = k.psum([128, 512], F32)
        pDs = [k.psum([128, 512], F32) for _ in range(2)]
        pD = pDs[0]
        pT = [k.psum([128, 1024], BF16) for _ in range(1)]
        rot = {"A": 0, "B": 0, "T": 0}

        def psA():
            rot["A"] += 1
            return pA[rot["A"] % 2]

        def psB():
            rot["B"] += 1
            return pB[rot["B"] % 2]

        def psT():
            return pT[0]

        tf = [k.sb([128, TT], F32) for _ in range(3)]
        tfi = [0]

        def tmpf():
            tfi[0] += 1
            return tf[tfi[0] % 3]

        tb = [k.sb([128, TT], BF16) for _ in range(3)]
        tbi = [0]

        def tmpb():
            tbi[0] += 1
            return tb[tbi[0] % 3]

        def mm_group(ps_ap, ps_reg, pairs, extra_reads=()):
            n = len(pairs)
            for i, (l_ap, r_ap, regs) in enumerate(pairs):
                op("pe", lambda h: h.matmul(ps_ap, lhsT=l_ap, rhs=r_ap, start=(i == 0), stop=(i == n - 1)),
                   reads=list(regs), writes=[ps_reg], inc=(i == n - 1))

        def proj(slot, blk, rhsbuf, N, ps, nk=KC, bw=512):
            sv = slot[:, :].rearrange("p (k c) -> p k c", k=nk)
            pairs = [(sv[:, kc, blk * 128:(blk + 1) * 128], rhsbuf[:, kc, 0:N], [slot.regs[0], rhsbuf.regs[kc]])
                     for kc in range(nk)]
            mm_group(ps[:, 0:N], ps.regs[0], pairs)

        def act(out, in_, func, reads, writes, bias=None, scale=None):
            kw = {}
            if bias is not None:
                kw["bias"] = bias
            if scale is not None:
                kw["scale"] = scale
            op("act", lambda h: h.activation(out=out, in_=in_, func=func, **kw), reads=reads, writes=writes)

        def tt(out, in0, in1, o, reads, writes, eng="dve"):
            op(eng, lambda h: h.tensor_tensor(out=out, in0=in0, in1=in1, op=o), reads=reads, writes=writes)

        def stt(out, in0, scalar, in1, op0, op1, reads, writes):
            op("dve", lambda h: h.scalar_tensor_tensor(out=out, in0=in0, scalar=scalar, in1=in1, op0=op0, op1=op1),
               reads=reads, writes=writes)

        def ts(out, in0, s1, s2, op0, op1, reads, writes):
            if op1 is None:
                op("dve", lambda h: h.tensor_scalar(out=out, in0=in0, scalar1=s1, scalar2=None, op0=op0), reads=reads, writes=writes)
            else:
                op("dve", lambda h: h.tensor_scalar(out=out, in0=in0, scalar1=s1, scalar2=s2, op0=op0, op1=op1), reads=reads, writes=writes)

        def rstd_from(ps, N, scale, dst):
            act(dst[:, 0:N], ps[:, 0:N], AF.Ln, ps.regs, dst.regs, bias=epsc[:, 0:1], scale=scale)
            act(dst[:, 0:N], dst[:, 0:N], AF.Exp, dst.regs, dst.regs, scale=-0.5)

        epsc = k.sb([128, 1], F32)
        op("dve", lambda h: h.memset(epsc[:, :], EPS), writes=epsc.regs)

        def pc(l, col):
            return pcol[:, l * 128 + col:l * 128 + col + 1]

        def prenorm(l, N, xb):
            with ExitStack() as esq:
                sqb = k.sb([128, KC, TT], BF16, KC, es=esq)
                for kc in range(KC):
                    act(sqb[:, kc, 0:N], xb[:, kc, 0:N], AF.Square, [xb.regs[kc]], [sqb.regs[kc]])
                ps = pC
                mm_group(ps[:, 0:N], ps.regs[0], [(onesb[:, :], sqb[:, kc, 0:N], [onesb.regs[0], sqb.regs[kc]]) for kc in range(KC)])
            rs = tmpf()
            rstd_from(ps, N, 1.0 / D, rs)
            for kc in range(KC):
                stt(hT[:, kc, 0:N], xb[:, kc, 0:N], pc(l, PC_NPRE + kc), rs[:, 0:N], ALU.mult, ALU.mult,
                    [xb.regs[kc], rs.regs[0], pcol.regs[0]], [hT.regs[kc]])

        def layer_tile(S, l, N, Q, first, last):
            nch = N // Q
            if getattr(S, 'prenorm_done', False):
                S.prenorm_done = False
            else:
                prenorm(l, N, BUFS['x'])

            stage('prenorm')
            with ExitStack() as esA:
                ubuf = k.sb([128, KC, 30 + TT], BF16, KC, es=esA)
                cc = k.sb([128, KC, TT], F32, KC, es=esA)
                ccb = k.sb([128, KC, TT], BF16, KC, es=esA)
                scg = k.sb([128, KC, TT], BF16, KC, es=esA)
                stA = [k.sb([128, TT], F32, es=esA) for _ in range(4)]
                op("act", lambda h: h.activation(out=ubuf[:, :, 0:30], in_=halo_a[l][:, :, :], func=AF.Copy),
                   reads=halo_a[l].regs, writes=ubuf.regs)
                for half in range(2):
                    sv = wslot(l, S_GV[half]); sg_ = wslot(l, S_GG[half])
                    for b in range(4):
                        c = half * 4 + b
                        p1 = psA(); p2 = psB()
                        proj(sv, b, hT, N, p1)
                        proj(sg_, b, hT, N, p2)
                        sgm = tmpf()
                        act(sgm[:, 0:N], p2[:, 0:N], AF.Sigmoid, p2.regs, sgm.regs)
                        tt(ubuf[:, c, 30:30 + N], p1[:, 0:N], sgm[:, 0:N], ALU.mult, p1.regs + sgm.regs, [ubuf.regs[c]])
                        if last:
                            tt(ulast[l][:, c, :], p1[:, N - 30:N], sgm[:, N - 30:N], ALU.mult, p1.regs + sgm.regs, ulast[l].regs)
                op("act", lambda h: h.activation(out=halo_a[l][:, :, :], in_=ubuf[:, :, N:N + 30], func=AF.Copy),
                   reads=ubuf.regs, writes=halo_a[l].regs)
                pend_stats = []
                for c in range(KC):
                    sd = wslot(l, S_DA + c)
                    dv = sd[:, :].rearrange("p (w j) -> p w j", w=32)
                    p1 = psA()
                    mm_group(p1[:, 0:N], p1.regs[0],
                             [(dv[:, w, :], ubuf[:, c, w:w + N], [sd.regs[0], ubuf.regs[c]]) for w in range(31)])
                    ts(cc[:, c, 0:N], p1[:, 0:N], pc(l, PC_DWB + c), None, ALU.add, None, p1.regs + pcol.regs, [cc.regs[c]])
                    act(ccb[:, c, 0:N], p1[:, 0:N], AF.Identity, p1.regs + pcol.regs, [ccb.regs[c]], bias=pc(l, PC_DWB + c))
                    sq = tmpb()
                    act(sq[:, 0:N], p1[:, 0:N], AF.Square, p1.regs + pcol.regs, sq.regs, bias=pc(l, PC_DWB + c))
                    def _stats(c=c, sq=sq):
                        op("pe", lambda h: h.matmul(pC[:, 0:N], lhsT=onesb[:, :], rhs=ccb[:, c, 0:N], start=(c == 0), stop=(c == KC - 1)),
                           reads=[onesb.regs[0], ccb.regs[c]], writes=pC.regs, inc=True)
                        op("pe", lambda h: h.matmul(pD[:, 0:N], lhsT=onesb[:, :], rhs=sq[:, 0:N], start=(c == 0), stop=(c == KC - 1)),
                           reads=[onesb.regs[0], sq.regs[0]], writes=pD.regs, inc=True)
                    if pend_stats:
                        pend_stats.pop()()
                    pend_stats.append(_stats)
                pend_stats.pop()()
                mean, msq, Ai, Bm = stA
                ts(mean[:, 0:N], pC[:, 0:N], 1.0 / D, None, ALU.mult, None, pC.regs, mean.regs)
                tt(msq[:, 0:N], mean[:, 0:N], mean[:, 0:N], ALU.mult, mean.regs, msq.regs)
                stt(msq[:, 0:N], pD[:, 0:N], 1.0 / D, msq[:, 0:N], ALU.mult, ALU.subtract, pD.regs + msq.regs, msq.regs)
                act(Ai[:, 0:N], msq[:, 0:N], AF.Ln, msq.regs, Ai.regs, bias=epsc[:, 0:1])
                act(Ai[:, 0:N], Ai[:, 0:N], AF.Exp, Ai.regs, Ai.regs, scale=-0.5)
                stt(Bm[:, 0:N], mean[:, 0:N], -1.0, Ai[:, 0:N], ALU.mult, ALU.mult, mean.regs + Ai.regs, Bm.regs)
                for half in range(2):
                    sw = wslot(l, S_CG[half])
                    for b in range(4):
                        c = half * 4 + b
                        p1 = psA()
                        proj(sw, b, hT, N, p1)
                        act(scg[:, c, 0:N], p1[:, 0:N], AF.Silu, p1.regs, [scg.regs[c]])
                t3s = {}
                for c in range(KC + 1):
                    if c < KC:
                        tt(cc[:, c, 0:N], cc[:, c, 0:N], Ai[:, 0:N], ALU.mult, [cc.regs[c]] + Ai.regs, [cc.regs[c]])
                        tt(cc[:, c, 0:N], cc[:, c, 0:N], Bm[:, 0:N], ALU.add, [cc.regs[c]] + Bm.regs, [cc.regs[c]])
                        t3 = tmpb()
                        t3s[c] = t3
                        act(t3[:, 0:N], cc[:, c, 0:N], AF.Silu, [cc.regs[c]] + pcol.regs, t3.regs,
                            bias=pc(l, PC_LNB + c), scale=pc(l, PC_LNW + c))
                    if c >= 1:
                        t3 = t3s.pop(c - 1)
                        tt(scg[:, c - 1, 0:N], t3[:, 0:N], scg[:, c - 1, 0:N], ALU.mult, t3.regs + [scg.regs[c - 1]], [scg.regs[c - 1]])
                gbufa = k.sb([128, KC, TT], F32, KC, es=esA)
                for gi in range(2):
                    gsl = wslot(l, S_G0[gi])
                    for bb in range(4):
                        c = gi * 4 + bb
                        p2 = psB()
                        proj(gsl, bb, hT, N, p2)
                        act(gbufa[:, c, 0:N], p2[:, 0:N], AF.Sigmoid, p2.regs + pcol.regs, [gbufa.regs[c]], bias=pc(l, PC_GB + c))
                for oi in range(2):
                    so = wslot(l, S_CO[oi])
                    for bb in range(4):
                        c = oi * 4 + bb
                        p1 = psA()
                        proj(so, bb, scg, N, p1)
                        tt(BUFS['m'][:, c, 0:N], p1[:, 0:N], gbufa[:, c, 0:N], ALU.mult, p1.regs + [gbufa.regs[c]], [BUFS['m'].regs[c]])
                k.bury([ubuf, cc, ccb, scg] + stA)

            stage('A')
            with ExitStack() as esB:
                sz = k.sb([128, 16, TT], BF16, 16, es=esB)
                NW = nch * 32
                dts = k.sb([128, 8, NW], F32, 1, es=esB)
                cdt = k.sb([128, NW], F32, 1, es=esB, own_block=True)
                dab16 = k.sb([128, NW], BF16, 1, es=esB, own_block=True)
                dal16 = k.sb([128, NW], BF16, 1, es=esB, own_block=True)
                eacs = k.sb([128, NW], F32, 1, es=esB, own_block=True)
                esXC = ExitStack()
                xc = k.sb([128, 24, TT], BF16, 24, es=esXC)
                esX = ExitStack()
                xbuf = k.sb([128, 8, 3 + TT], BF16, 8, es=esX)
                dr = dts.regs
                v_, ax, ee, dt_, da, nacs, dte, dd = [dts[:, i, :] for i in range(8)]
                pd = psB()

                def v3(ap):
                    return ap.rearrange("p (c h) -> p c h", c=nch)

                def bc3(ap):
                    return ap.unsqueeze(1).broadcast_to([Q, nch, 32])

                def dt_part1():
                    for qi in range(nch):
                        cs = slice(qi * Q, (qi + 1) * Q)
                        mm_group(pd[0:Q, qi * 32:(qi + 1) * 32], pd.regs[0],
                                 [(hT[:, kc, cs], wdtb[:, l, kc, :], [hT.regs[kc], wdtb.regs[0]]) for kc in range(KC)])

                def dt_chain1():
                    tt(v3(v_[0:Q]), v3(pd[0:Q, 0:NW]), bc3(prow[0:Q, l * 64:l * 64 + 32]), ALU.add, [pd.regs[0]] + prow.regs, dr)
                    ts(ax[0:Q], v_[0:Q], -1.0, None, ALU.mult, None, dr, dr)
                    tt(ax[0:Q], ax[0:Q], v_[0:Q], ALU.max, dr, dr)
                    act(ee[0:Q], ax[0:Q], AF.Exp, dr, dr, scale=-1.0)
                    act(ee[0:Q], ee[0:Q], AF.Ln, dr, dr, bias=1.0)
                    stt(dt_[0:Q], v_[0:Q], 0.0, ee[0:Q], ALU.max, ALU.add, dr, dr)
                    tt(v3(da[0:Q]), v3(dt_[0:Q]), bc3(arow[0:Q, l * 32:(l + 1) * 32]), ALU.mult, dr + arow.regs, dr)
                    act(dab16[0:Q, :], da[0:Q], AF.Copy, dr, dab16.regs)
                    tt(ee[0:Q], da[0:Q], dab16[0:Q, :], ALU.subtract, dr + dab16.regs, dr)
                    act(dal16[0:Q, :], ee[0:Q], AF.Copy, dr, dal16.regs)
                    mm_group(pd[0:Q, 128:128 + NW], pd.regs[0], [(trif[0:Q, 0:Q], da[0:Q], dr + cstf.regs)])
                    mm_group(pd[0:128, 256:256 + NW], pd.regs[0], [(onesf[0:Q, 0:128], da[0:Q], dr + cstf.regs)])

                def dt_chain2():
                    ts(nacs[0:Q], pd[0:Q, 128:128 + NW], -1.0, None, ALU.mult, None, [pd.regs[0]], dr)
                    act(eacs[0:Q, :], nacs[0:Q], AF.Exp, dr, eacs.regs, scale=-1.0)
                    tt(dd[0:Q], pd[0:Q, 256:256 + NW], nacs[0:Q], ALU.add, [pd.regs[0]] + dr, dr)
                    act(dte[0:Q], dd[0:Q], AF.Exp, dr, dr)
                    act(cdt[:, :], pd[0:128, 256:256 + NW], AF.Exp, [pd.regs[0]], cdt.regs)
                    tt(dte[0:Q], dte[0:Q], dt_[0:Q], ALU.mult, dr, dr)
                    ts(ax[0:Q], dt_[0:Q], 1e-30, None, ALU.max, None, dr, dr)
                    act(ax[0:Q], ax[0:Q], AF.Ln, dr, dr)
                    tt(nacs[0:Q], nacs[0:Q], ax[0:Q], ALU.add, dr, dr)

                dt_part1()
                for zi in range(4):
                    sw = wslot(l, S_Z + zi)
                    for b in range(4):
                        j = zi * 4 + b
                        p1 = psA()
                        proj(sw, b, hT, N, p1)
                        act(sz[:, j, 0:N], p1[:, 0:N], AF.Silu, p1.regs, [sz.regs[j]])
                    if zi == 1:
                        dt_chain1()
                    if zi == 3:
                        dt_chain2()
                for gi in range(3):
                    op("act", lambda h: h.activation(out=xbuf[:, :, 0:3], in_=halo_s[l][:, gi * 8:(gi + 1) * 8, :], func=AF.Copy),
                       reads=halo_s[l].regs, writes=xbuf.regs)
                    for half in range(2):
                        sw = wslot(l, S_XBC[gi][half])
                        for b in range(4):
                            jj = half * 4 + b
                            p1 = psA()
                            proj(sw, b, hT, N, p1)
                            act(xbuf[:, jj, 3:3 + N], p1[:, 0:N], AF.Copy, p1.regs, [xbuf.regs[jj]])
                            if last:
                                op("dve", lambda h: h.tensor_copy(out=xlast[l][:, gi * 8 + jj, :], in_=p1[:, N - 3:N]),
                                   reads=p1.regs, writes=xlast[l].regs)
                    op("act", lambda h: h.activation(out=halo_s[l][:, gi * 8:(gi + 1) * 8, :], in_=xbuf[:, :, N:N + 3], func=AF.Copy),
                       reads=xbuf.regs, writes=halo_s[l].regs)
                    sd = wslot(l, S_DS[gi])
                    dv = sd[:, :].rearrange("p (j w c) -> p j w c", j=8, w=4)
                    for jj in range(8):
                        j = gi * 8 + jj
                        p1 = psB()
                        mm_group(p1[:, 0:N], p1.regs[0],
                                 [(dv[:, jj, w, :], xbuf[:, jj, w:w + N], [sd.regs[0], xbuf.regs[jj]]) for w in range(4)])
                        act(xc[:, j, 0:N], p1[:, 0:N], AF.Silu, p1.regs + pcol.regs, [xc.regs[j]], bias=pc(l, PC_SCB + j))
                k.bury([xbuf])
                esX.close()
                stage('Bconv')
                esS = ExitStack()
                yoff = k.sb([128, 2048], BF16, 4, es=esS)
                xtm = k.sb([128, 2048], BF16, 2, es=esS)
                xdt2 = k.sb([128, 2048], BF16, 2, es=esS)
                btm = k.sb([128, 512], BF16, es=esS)
                LTs = [k.sb([128, 128], BF16, es=esS, own_block=True) for _ in range(3)]
                MT = k.sb([128, 32, 128], BF16, 32, es=esS)
                for qd in range(4):
                    act(hstb[:, qd * 512:(qd + 1) * 512], hst[l][:, qd * 512:(qd + 1) * 512], AF.Copy, [hst[l].regs[qd]], [hstb.regs[qd]])
                for qi in range(nch):
                    cs = slice(qi * Q, (qi + 1) * Q)
                    co = qi * 32
                    for hf in range(2):
                        ptb = (pT[0], pDs[0])[hf]
                        pt = Buf(ptb[:, :] if hf == 0 else ptb[:, :].bitcast(BF16), ptb.regs)
                        for jj in range(8):
                            j = hf * 8 + jj
                            op("pe", lambda h: h.transpose(out=pt[0:Q, jj * 128:(jj + 1) * 128], in_=xc[:, j, cs], identity=identb[:, :]),
                               reads=[xc.regs[j], identb.regs[0]], writes=pt.regs, inc=(jj == 7))
                        pv = pt[0:Q, :].rearrange("p (h d) -> p h d", h=16)
                        ov2 = xdt2[0:Q, hf * 1024:(hf + 1) * 1024].rearrange("p (h d) -> p h d", h=16)
                        act(xtm[0:Q, hf * 1024:(hf + 1) * 1024], pt[0:Q, :], AF.Copy, pt.regs, [xtm.regs[hf]])
                        tt(ov2, pv, bc_last(dte[0:Q, co + hf * 16:co + (hf + 1) * 16], 64), ALU.mult, pt.regs + dr, [xdt2.regs[hf]])
                    pt = Buf(pDs[1][:, :].bitcast(BF16), pDs[1].regs)
                    for g in range(4):
                        op("pe", lambda h: h.transpose(out=pt[0:Q, g * 128:(g + 1) * 128], in_=xc[:, 16 + g, cs], identity=identb[:, :]),
                           reads=[xc.regs[16 + g], identb.regs[0]], writes=pt.regs, inc=(g == 3))
                    act(btm[0:Q, :], pt[0:Q, 0:512], AF.Copy, pt.regs, btm.regs)
                    for g in range(4):
                        op("pe", lambda h: h.matmul(pC[0:Q, g * 128:g * 128 + Q], lhsT=xc[:, 16 + g, cs], rhs=xc[:, 20 + g, cs], start=True, stop=True),
                           reads=[xc.regs[16 + g], xc.regs[20 + g]], writes=pC.regs, inc=(g == 3))
                    for hb in range(1):
                        for h16 in range(32):
                            hh = h16
                            g = hh // 8
                            pDh = (pA[0], pA[1], pB[0], pB[1], pDs[0], pDs[1])[hh % 6]
                            dab = dab16[0:Q, co + hh:co + hh + 1]
                            dlb = dal16[0:Q, co + hh:co + hh + 1]
                            op("pe", lambda h: h.matmul(pDh[0:Q, 0:Q], lhsT=identb[0:Q, 0:Q], rhs=masknegb[0:Q, 0:Q], start=True, stop=False),
                               reads=[identb.regs[0], masknegb.regs[0]], writes=pDh.regs, inc=False)
                            op("pe", lambda h: h.matmul(pDh[0:Q, 0:Q], lhsT=dab.broadcast_to([Q, Q]), rhs=trib[0:Q, 0:Q], start=False, stop=False),
                               reads=dab16.regs + trib.regs, writes=pDh.regs, inc=False)
                            op("pe", lambda h: h.matmul(pDh[0:Q, 0:Q], lhsT=dlb.broadcast_to([Q, Q]), rhs=trib[0:Q, 0:Q], start=False, stop=True),
                               reads=dal16.regs + trib.regs, writes=pDh.regs, inc=True)
                            LT = LTs[hh % 3]
                            act(LT[0:Q, 0:Q], pDh[0:Q, 0:Q], AF.Exp, pDh.regs + dr, LT.regs, bias=nacs[0:Q, co + hh:co + hh + 1])
                            tt(MT[0:Q, h16, 0:Q], pC[0:Q, g * 128:g * 128 + Q], LT[0:Q, 0:Q], ALU.mult, pC.regs + LT.regs, [MT.regs[h16]])
                        for g in range(4):
                            pr = psB()
                            mm_group(pr[0:Q, 0:512], pr.regs[0], [(xc[:, 20 + g, cs], hstb[:, g * 512:(g + 1) * 512], [xc.regs[20 + g], hstb.regs[g]])])
                            tt(yoff[0:Q, g * 512:(g + 1) * 512].rearrange("p (h d) -> p h d", h=8),
                               pr[0:Q, 0:512].rearrange("p (h d) -> p h d", h=8),
                               bc_last(eacs[0:Q, co + g * 8:co + (g + 1) * 8], 64), ALU.mult, pr.regs + eacs.regs, [yoff.regs[g]])
                        for j in range(16):
                            py = psA()
                            for h2 in range(2):
                                hh = 2 * j + h2
                                h16 = hh
                                op("pe", lambda h: h.matmul(py[h2 * 64:(h2 + 1) * 64, 0:Q], lhsT=xtm[0:Q, hh * 64:(hh + 1) * 64], rhs=MT[0:Q, h16, 0:Q], start=True, stop=False),
                                   reads=[xtm.regs[hh // 16], MT.regs[h16]], writes=py.regs, inc=False)
                                op("pe", lambda h: h.matmul(py[h2 * 64:(h2 + 1) * 64, 0:Q], lhsT=yoff[0:Q, hh * 64:(hh + 1) * 64], rhs=identb[0:Q, 0:Q], start=False, stop=True),
                                   reads=[yoff.regs[hh // 8], identb.regs[0]], writes=py.regs, inc=(h2 == 1))
                            yt = tmpf()
                            stt(yt[:, 0:Q], xc[:, j, cs], pc(l, PC_D + j), py[:, 0:Q], ALU.mult, ALU.add, [xc.regs[j]] + pcol.regs + py.regs, yt.regs)
                            tt(sz[:, j, cs], yt[:, 0:Q], sz[:, j, cs], ALU.mult, yt.regs + [sz.regs[j]], [sz.regs[j]])
                    for qd in range(4):
                        pss = psB()
                        for h8 in range(8):
                            hh = qd * 8 + h8
                            op("pe", lambda h: h.matmul(pss[:, h8 * 64:(h8 + 1) * 64], lhsT=btm[0:Q, qd * 128:(qd + 1) * 128], rhs=xdt2[0:Q, hh * 64:(hh + 1) * 64], start=True, stop=True),
                               reads=btm.regs + [xdt2.regs[hh // 16]], writes=pss.regs, inc=(h8 == 7))
                        hv = hst[l][:, qd * 512:(qd + 1) * 512].rearrange("p (h d) -> p h d", h=8)
                        tt(hv, hv, bc_last(cdt[:, co + qd * 8:co + (qd + 1) * 8], 64), ALU.mult, [hst[l].regs[qd]] + cdt.regs, [hst[l].regs[qd]])
                        tt(hst[l][:, qd * 512:(qd + 1) * 512], hst[l][:, qd * 512:(qd + 1) * 512], pss[:, :], ALU.add,
                           [hst[l].regs[qd]] + pss.regs, [hst[l].regs[qd]])
                        if qi < nch - 1:
                            act(hstb[:, qd * 512:(qd + 1) * 512], hst[l][:, qd * 512:(qd + 1) * 512], AF.Copy, [hst[l].regs[qd]], [hstb.regs[qd]])
                esS.close()
                esXC.close()
                rg = k.sb([128, 4, TT], F32, 4, es=esB)
                gbuf = k.sb([128, KC, TT], F32, KC, es=esB)
                nbanks = [pC, pDs[0], pDs[1], Buf(pT[0][:, :].bitcast(F32), pT[0].regs)]
                for g in range(4):
                    bk = nbanks[g]
                    for j4 in range(4):
                        j = g * 4 + j4
                        sq = tmpb()
                        act(sq[:, 0:N], sz[:, j, 0:N], AF.Square, [sz.regs[j]], sq.regs)
                        op("pe", lambda h: h.matmul(bk[:, 0:N], lhsT=onesb[:, :], rhs=sq[:, 0:N], start=(j4 == 0), stop=(j4 == 3)),
                           reads=[onesb.regs[0]] + sq.regs, writes=bk.regs, inc=True)
                for g in range(4):
                    bk = nbanks[g]
                    act(rg[:, g, 0:N], bk[:, 0:N], AF.Ln, bk.regs, [rg.regs[g]], bias=epsc[:, 0:1], scale=1.0 / 512)
                    act(rg[:, g, 0:N], rg[:, g, 0:N], AF.Exp, [rg.regs[g]], [rg.regs[g]], scale=-0.5)
                    for j4 in range(4):
                        j = g * 4 + j4
                        stt(sz[:, j, 0:N], sz[:, j, 0:N], pc(l, PC_SNW + j), rg[:, g, 0:N], ALU.mult, ALU.mult,
                            [sz.regs[j], rg.regs[g]] + pcol.regs, [sz.regs[j]])
                for gi in range(2):
                    gsl = wslot(l, S_G1[gi])
                    for bb in range(4):
                        c = gi * 4 + bb
                        p2 = psB()
                        proj(gsl, bb, hT, N, p2)
                        act(gbuf[:, c, 0:N], p2[:, 0:N], AF.Sigmoid, p2.regs + pcol.regs, [gbuf.regs[c]], bias=pc(l, PC_GB + 8 + c))
                for oi in range(4):
                    so = wslot(l, S_SO[oi])
                    for bb in range(2):
                        c = oi * 2 + bb
                        p1 = psA()
                        proj(so, bb, sz, N, p1, nk=16, bw=256)
                        g_ = tmpf()
                        tt(g_[:, 0:N], p1[:, 0:N], gbuf[:, c, 0:N], ALU.mult, p1.regs + [gbuf.regs[c]], g_.regs)
                        tt(BUFS['m'][:, c, 0:N], BUFS['m'][:, c, 0:N], g_[:, 0:N], ALU.add, [BUFS['m'].regs[c]] + g_.regs, [BUFS['m'].regs[c]])

            stage('B')
            with ExitStack() as esC:
                qT = k.sb([128, KC, TT], BF16, KC, es=esC)
                sxg = k.sb([128, KC, TT], BF16, KC, es=esC)
                Et = [k.sb([128, 2, TT], BF16, es=esC) for _ in range(2)]
                kv = k.sb([128, SLOTW], BF16, es=esC)
                if S.kvi == 0:
                    k.dma("sp", kv[:, :], kvb[0, l], kv_sl, reads=[kvb_reg[0][l]], writes=kv.regs)
                else:
                    cast_into(kv, kvc[S.kvi - 1, l], kv_sl)
                kTv = kv[:, 0:2048].rearrange("p (h c m) -> p h c m", h=4, c=2)
                vv = kv[:, 2048:4096].rearrange("p (c f) -> p c f", c=2)
                for half in range(2):
                    sw = wslot(l, S_Q[half])
                    for b in range(4):
                        c = half * 4 + b
                        p1 = psA()
                        proj(sw, b, hT, N, p1)
                        act(qT[:, c, 0:N], p1[:, 0:N], AF.Copy, p1.regs, [qT.regs[c]])
                for half in range(2):
                    sw = wslot(l, S_XG[half])
                    for b in range(4):
                        c = half * 4 + b
                        p1 = psA()
                        proj(sw, b, hT, N, p1)
                        act(sxg[:, c, 0:N], p1[:, 0:N], AF.Silu, p1.regs, [sxg.regs[c]])
                for hh in range(4):
                    et = Et[hh % 2]
                    for mc in range(2):
                        p1 = psA()
                        mm_group(p1[:, 0:N], p1.regs[0],
                                 [(kTv[:, hh, dc, mc * 128:(mc + 1) * 128], qT[:, hh * 2 + dc, 0:N], [kv.regs[0], qT.regs[hh * 2 + dc]]) for dc in range(2)])
                        act(et[:, mc, 0:N], p1[:, 0:N], AF.Exp, p1.regs, et.regs, scale=1.0 / 16.0)
                    p2 = psB()
                    mm_group(p2[:, 0:N], p2.regs[0], [(onesb[:, :], et[:, mc, 0:N], [onesb.regs[0]] + et.regs) for mc in range(2)])
                    rden = tmpf()
                    act(rden[:, 0:N], p2[:, 0:N], AF.Ln, p2.regs, rden.regs)
                    act(rden[:, 0:N], rden[:, 0:N], AF.Exp, rden.regs, rden.regs, scale=-1.0)
                    for dc in range(2):
                        c = hh * 2 + dc
                        p1 = psA()
                        mm_group(p1[:, 0:N], p1.regs[0],
                                 [(vv[:, mc, hh * 256 + dc * 128:hh * 256 + (dc + 1) * 128], et[:, mc, 0:N], [kv.regs[0]] + et.regs) for mc in range(2)])
                        ot = tmpf()
                        tt(ot[:, 0:N], p1[:, 0:N], rden[:, 0:N], ALU.mult, p1.regs + rden.regs, ot.regs)
                        tt(sxg[:, c, 0:N], ot[:, 0:N], sxg[:, c, 0:N], ALU.mult, ot.regs + [sxg.regs[c]], [sxg.regs[c]])
                gbufc = k.sb([128, KC, TT], F32, KC, es=esC)
                for gi in range(2):
                    gsl = wslot(l, S_G2[gi])
                    for bb in range(4):
                        c = gi * 4 + bb
                        p2 = psB()
                        proj(gsl, bb, hT, N, p2)
                        act(gbufc[:, c, 0:N], p2[:, 0:N], AF.Sigmoid, p2.regs + pcol.regs, [gbufc.regs[c]], bias=pc(l, PC_GB + 16 + c))
                for oi in range(2):
                    so = wslot(l, S_XO[oi])
                    for bb in range(4):
                        c = oi * 4 + bb
                        p1 = psA()
                        proj(so, bb, sxg, N, p1)
                        g_ = tmpf()
                        tt(g_[:, 0:N], p1[:, 0:N], gbufc[:, c, 0:N], ALU.mult, p1.regs + [gbufc.regs[c]], g_.regs)
                        tt(BUFS['m'][:, c, 0:N], BUFS['m'][:, c, 0:N], g_[:, 0:N], ALU.add, [BUFS['m'].regs[c]] + g_.regs, [BUFS['m'].regs[c]])
                k.bury([qT, sxg, kv] + Et)

            stage('C')
            with ExitStack() as esO:
                mb = k.sb([128, KC, TT], BF16, KC, es=esO)
                o32 = k.sb([128, KC, TT], F32, KC, es=esO)
                for c in range(KC):
                    act(mb[:, c, 0:N], BUFS['m'][:, c, 0:N], AF.Copy, [BUFS['m'].regs[c]], [mb.regs[c]])
                if getattr(S, 'prefetch', None):
                    S.prefetch()
                    S.prefetch = None
                for half in range(2):
                    sw = wslot(l, S_WO[half])
                    for b in range(4):
                        c = half * 4 + b
                        p1 = psA()
                        proj(sw, b, mb, N, p1)
                        sq = tmpb()
                        act(sq[:, 0:N], p1[:, 0:N], AF.Square, p1.regs, sq.regs)
                        op("dve", lambda h: h.tensor_copy(out=o32[:, c, 0:N], in_=p1[:, 0:N]), reads=p1.regs, writes=[o32.regs[c]])
                        op("pe", lambda h: h.matmul(pC[:, 0:N], lhsT=onesb[:, :], rhs=sq[:, 0:N], start=(c == 0), stop=(c == KC - 1)),
                           reads=[onesb.regs[0]] + sq.regs, writes=pC.regs, inc=True)
                rr = tmpf()
                rstd_from(pC, N, 1.0 / D, rr)
                if getattr(S, 'next_prenorm', None):
                    S.next_prenorm()
                    S.next_prenorm = None
                for c in range(KC):
                    stt(o32[:, c, 0:N], o32[:, c, 0:N], pc(l, PC_NPOST + c), rr[:, 0:N], ALU.mult, ALU.mult,
                        [o32.regs[c]] + rr.regs + pcol.regs, [o32.regs[c]])
                    tt(BUFS['x'][:, c, 0:N], BUFS['x'][:, c, 0:N], o32[:, c, 0:N], ALU.add, [BUFS['x'].regs[c], o32.regs[c]], [BUFS['x'].regs[c]])
                k.bury([mb, o32])

        def merge_branch(l, bi, N, s_out, s_g, src, nk, bw):
            nblk = bw // 128
            ci = 0
            gs = [None, None]
            order = []
            per_g = 4 // nblk
            for gi in range(2):
                outs = []
                for oi in range(per_g):
                    outs.append(wslot(l, s_out[gi * per_g + oi]))
                gsl = wslot(l, s_g[gi])
                for b in range(4):
                    c = gi * 4 + b
                    so = outs[b // nblk]
                    p1 = psA(); p2 = psB()
                    proj(so, b % nblk, src, N, p1, nk=nk, bw=bw)
                    proj(gsl, b, hT, N, p2)
                    g_ = tmpf()
                    act(g_[:, 0:N], p2[:, 0:N], AF.Sigmoid, p2.regs + pcol.regs, g_.regs, bias=pc(l, PC_GB + bi * 8 + c))
                    if bi == 0:
                        tt(BUFS['m'][:, c, 0:N], p1[:, 0:N], g_[:, 0:N], ALU.mult, p1.regs + g_.regs, [BUFS['m'].regs[c]])
                    else:
                        tt(g_[:, 0:N], p1[:, 0:N], g_[:, 0:N], ALU.mult, p1.regs + g_.regs, g_.regs)
                        tt(BUFS['m'][:, c, 0:N], BUFS['m'][:, c, 0:N], g_[:, 0:N], ALU.add, [BUFS['m'].regs[c]] + g_.regs, [BUFS['m'].regs[c]])

        def run_seq(kind, si):
            S = Seq()
            S.kvi = 0 if kind == "p" else 1 + si
            if kind == "p":
                Tn, N, Q = T_PROMPT, TT_P, Q_P
            else:
                Tn, N, Q = T_S, T_S, T_S
            ntile = Tn // N
            if kind == "p":
                for l in range(2):
                    op("dve", lambda h: h.memset(hst[l][:, :], 0.0), writes=hst[l].regs)
                    op("dve", lambda h: h.memset(halo_a[l][:, :, :], 0.0), writes=halo_a[l].regs)
                    op("dve", lambda h: h.memset(halo_s[l][:, :, :], 0.0), writes=halo_s[l].regs)
                with ExitStack() as esK:
                    memx = k.sb([128, KC, 256], F32, es=esK)
                    memq = k.sb([128, KC, 256], BF16, es=esK)
                    memn = k.sb([128, KC, 256], BF16, KC, es=esK)
                    kv32 = k.sb([128, SLOTW], F32, es=esK)
                    kvbf = k.sb([128, SLOTW], BF16, es=esK)
                    k.dma("pool", memx[:, :, :], mem_d, ioslot(), writes=memx.regs)
                    stage('kv0')
                    act(memq[:, :, :], memx[:, :, :], AF.Square, memx.regs, memq.regs)
                    mm_group(pC[:, 0:256], pC.regs[0], [(onesb[:, :], memq[:, kc, :], [onesb.regs[0]] + memq.regs) for kc in range(KC)])
                    stage('kv1')
                    rm = tmpf()
                    rstd_from(pC, 256, 1.0 / D, rm)
                    stage('kv2')
                    for l in range(2):
                        for kc in range(KC):
                            stt(memn[:, kc, :], memx[:, kc, :], pc(l, PC_MNW + kc), rm[:, 0:256], ALU.mult, ALU.mult,
                                memx.regs + rm.regs + pcol.regs, [memn.regs[kc]])
                        stage('kv3')
                        for half in range(2):
                            sw = wslot(l, S_KVK[half])
                            stage('kv3a')
                            for b in range(4):
                                blk = half * 4 + b
                                p1 = psA()
                                proj(sw, b, memn, 256, p1)
                                stage('kv3b')
                                act(kvbf[:, blk * 256:(blk + 1) * 256], p1[:, 0:256], AF.Copy, p1.regs, kvbf.regs)
                                stage('kv3b2')
                                op("dve", lambda h: h.tensor_copy(out=kv32[:, blk * 256:(blk + 1) * 256], in_=p1[:, 0:256]), reads=p1.regs, writes=kv32.regs)
                                stage('kv3b3')
                        stage('kv3c')
                        for half in range(2):
                            sw = wslot(l, S_KVV[half])
                            sv = sw[:, :].rearrange("p (k c) -> p k c", k=KC)
                            for mc in range(2):
                                p1 = psA()
                                mm_group(p1[:, 0:512], p1.regs[0],
                                         [(memn[:, kc, mc * 128:(mc + 1) * 128], sv[:, kc, :], [memn.regs[kc], sw.regs[0]]) for kc in range(KC)])
                                o0 = 2048 + mc * 1024 + half * 512
                                act(kvbf[:, o0:o0 + 512], p1[:, 0:512], AF.Copy, p1.regs, kvbf.regs)
                                op("dve", lambda h: h.tensor_copy(out=kv32[:, o0:o0 + 512], in_=p1[:, 0:512]), reads=p1.regs, writes=kv32.regs)
                        stage('kv4')
                        k.dma("pool", kvb[0, l], kvbf[:, :], ioslot(), reads=kvbf.regs, writes=[kvb_reg[0][l]])
                        k.dma("pool", mk_o[l], kv32[:, 0:2048], ioslot(), reads=kv32.regs)
                        k.dma("pool", mv_o[l], kv32[:, 2048:4096], ioslot(), reads=kv32.regs)
                    k.bury([memx, memq, memn, kv32, kvbf])
            else:
                with ExitStack() as esK:
                    ha = k.sb([128, KC, 30], F32, es=esK)
                    hs = k.sb([128, 24, 3], F32, es=esK)
                    for l in range(2):
                        k.dma("pool", hst[l][:, :], ssm_d[si, l], ioslot(), writes=hst[l].regs)
                        k.dma("pool", ha[:, :, :], sca_d[si, l], ioslot(), writes=ha.regs)
                        k.dma("pool", hs[:, :, :], scs_d[si, l], ioslot(), writes=hs.regs)
                        op("act", lambda h: h.activation(out=halo_a[l][:, :, :], in_=ha[:, :, :], func=AF.Copy), reads=ha.regs, writes=halo_a[l].regs)
                        op("act", lambda h: h.activation(out=halo_s[l][:, :, :], in_=hs[:, :, :], func=AF.Copy), reads=hs.regs, writes=halo_s[l].regs)
                    k.bury([ha, hs])
            stage('kv')
            for ti in range(ntile):
                if kind == "p":
                    src = xp_d[:, :, ti * N:(ti + 1) * N]
                    dst = yp_o[:, :, ti * N:(ti + 1) * N]
                else:
                    src = xs_d[si]
                    dst = ys_o[si]
                if not GS['preloaded']:
                    k.dma("pool", BUFS['x'][:, :, 0:N], src, ioslot(), writes=BUFS['x'].regs)
                GS['preloaded'] = False
                if GS['prenorm_done']:
                    S.prenorm_done = True
                    GS['prenorm_done'] = False
                if kind == "p" and ti + 1 < ntile:
                    nxt = (xp_d[:, :, (ti + 1) * N:(ti + 2) * N], N)
                elif kind == "p" and N_SAMP > 0:
                    nxt = (xs_d[0], T_S)
                elif kind == "s" and si + 1 < N_SAMP:
                    nxt = (xs_d[si + 1], T_S)
                else:
                    nxt = None
                for l in range(2):
                    if l == 1 and nxt is not None:
                        def _pf(nxt=nxt):
                            k.dma("pool", BUFS['m'][:, :, 0:nxt[1]], nxt[0], ioslot(), writes=BUFS['m'].regs)
                            GS['preloaded'] = True
                        S.prefetch = _pf

                        def _pn(nxt=nxt):
                            prenorm(0, nxt[1], BUFS['m'])
                            GS['prenorm_done'] = True
                        S.next_prenorm = _pn
                    if nxt is None and l == 1:
                        FINAL[0] = True
                    with ExitStack() as esl:
                        S.es_l = esl
                        layer_tile(S, l, N, Q, ti == 0, ti == ntile - 1)
                        LT_COUNT[0] += 1
                        stage('L%d' % LT_COUNT[0])
                k.dma("pool", dst, BUFS['x'][:, :, 0:N], ioslot(), reads=BUFS['x'].regs)
                if GS['preloaded']:
                    BUFS['x'], BUFS['m'] = BUFS['m'], BUFS['x']
            for l in range(2):
                if kind == "p":
                    oa, os_, oh = cap_o[l], csp_o[l], hp_o[l]
                else:
                    oa, os_, oh = cas_o[si, l], css_o[si, l], hs_o[si, l]
                k.dma("pool", oa, ulast[l][:, :, :], ioslot(), reads=ulast[l].regs)
                k.dma("pool", os_, xlast[l][:, :, :], ioslot(), reads=xlast[l].regs)
                k.dma("pool", oh, hst[l][:, :], ioslot(), reads=hst[l].regs)

        try:
            stage('prepass')
            run_seq("p", 0)
            for si in range(N_SAMP):
                run_seq("s", si)
        except _Stop as e_:
            print('STOPPED at', e_)
        k.finish()
        print("instructions:", k.ninst, "waits:", k.nwait, "arena peak", k.apeak)
    return nc


def _fm(x):
    T, C = x.shape
    return np.ascontiguousarray(x.T.reshape(C // 128, 128, T).transpose(1, 0, 2))


def _fm_inv(a):
    p, kc, T = a.shape
    return np.ascontiguousarray(a.transpose(1, 0, 2).reshape(kc * p, T).T)


def _slot_proj(W):
    K, N = W.shape
    return np.ascontiguousarray(W.reshape(K // 128, 128, N).transpose(1, 0, 2)).reshape(128, -1)


def _col(v):
    return np.ascontiguousarray(v.reshape(-1, 128).T)


def prep_weights(inp):
    f = np.float32
    WS = np.zeros((2, NS, 128, SLOTW), f)
    wdt = np.zeros((2, 128, KC * 32), f)
    pcol = np.zeros((128, 256), f)
    prow = np.zeros((128, 128), f)
    o = [0, 1024, 2048, 3072, 5120, 8192, 8224, 9248, 10272, 13344]
    for l in range(2):
        w = np.asarray(inp["w_in"][l])

        def sec(off, i):
            return _slot_proj(w[:, off + i * 512: off + (i + 1) * 512])
        for i in range(2):
            WS[l, S_GV[i]] = sec(o[0], i)
            WS[l, S_GG[i]] = sec(o[1], i)
            WS[l, S_CG[i]] = sec(o[2], i)
            WS[l, S_Q[i]] = sec(o[6], i)
            WS[l, S_XG[i]] = sec(o[7], i)
            WS[l, S_G0[i]] = sec(o[8], i)
            WS[l, S_G1[i]] = sec(o[8] + 1024, i)
            WS[l, S_G2[i]] = sec(o[8] + 2048, i)
            WS[l, S_CO[i]] = _slot_proj(np.asarray(inp["conv_out_w"][l])[:, i * 512:(i + 1) * 512])
            WS[l, S_XO[i]] = _slot_proj(np.asarray(inp["xa_out_w"][l])[:, i * 512:(i + 1) * 512])
            WS[l, S_WO[i]] = _slot_proj(np.asarray(inp["w_out"][l])[:, i * 512:(i + 1) * 512])
            WS[l, S_KVK[i]] = _slot_proj(np.asarray(inp["xa_kv_w"][l])[:, i * 512:(i + 1) * 512])
            WS[l, S_KVV[i]] = _slot_proj(np.asarray(inp["xa_kv_w"][l])[:, 1024 + i * 512:1024 + (i + 1) * 512])
        for i in range(4):
            WS[l, S_Z + i] = sec(o[3], i)
            WS[l, S_SO[i]] = _slot_proj(np.asarray(inp["ssd_out_w"][l])[:, i * 256:(i + 1) * 256])
        for gi in range(3):
            for hf in range(2):
                WS[l, S_XBC[gi][hf]] = sec(o[4], gi * 2 + hf)
        wdt[l] = _slot_proj(w[:, o[5]:o[5] + 32])
        cw = np.asarray(inp["conv_dw_w"][l])
        idx = np.arange(128)
        for c in range(KC):
            a = np.zeros((128, 32, 128), f)
            a[idx, :31, idx] = cw[:, c * 128:(c + 1) * 128].T
            WS[l, S_DA + c] = a.reshape(128, -1)
        sw = np.asarray(inp["ssd_conv_w"][l])
        for gi in range(3):
            a = np.zeros((128, 8, 4, 128), f)
            for jj in range(8):
                j = gi * 8 + jj
                a[idx, jj, :, idx] = sw[:, j * 128:(j + 1) * 128].T
            WS[l, S_DS[gi]] = a.reshape(128, -1)
        b = l * 128
        pcol[:, b + PC_NPRE:b + PC_NPRE + 8] = _col(np.asarray(inp["norm_pre_w"][l]))
        pcol[:, b + PC_GB:b + PC_GB + 24] = _col(np.asarray(inp["gate_b"][l]))
        pcol[:, b + PC_DWB:b + PC_DWB + 8] = _col(np.asarray(inp["conv_dw_b"][l]))
        pcol[:, b + PC_LNW:b + PC_LNW + 8] = _col(np.asarray(inp["conv_ln_w"][l]))
        pcol[:, b + PC_LNB:b + PC_LNB + 8] = _col(np.asarray(inp["conv_ln_b"][l]))
        pcol[:, b + PC_SCB:b + PC_SCB + 24] = _col(np.asarray(inp["ssd_conv_b"][l]))
        pcol[:, b + PC_D:b + PC_D + 16] = _col(np.repeat(np.asarray(inp["ssd_d"][l]), 64))
        pcol[:, b + PC_SNW:b + PC_SNW + 16] = _col(np.asarray(inp["ssd_norm_w"][l]))
        pcol[:, b + PC_NPOST:b + PC_NPOST + 8] = _col(np.asarray(inp["norm_post_w"][l]))
        pcol[:, b + PC_MNW:b + PC_MNW + 8] = _col(np.asarray(inp["mem_norm_w"][l]))
        prow[:, l * 64:l * 64 + 32] = np.asarray(inp["ssd_dt_bias"][l])[None, :]
        prow[:, l * 64 + 32:l * 64 + 64] = np.asarray(inp["ssd_a_log"][l])[None, :]
    cst = np.zeros((128, 512), f)
    i = np.arange(128)
    cst[:, 0:128] = np.eye(128, dtype=f)
    cst[:, 128:256] = (i[:, None] <= i[None, :]).astype(f)
    cst[:, 256:384] = np.where(i[None, :] < i[:, None], -30000.0, 0.0).astype(f)
    cst[:, 384:512] = 1.0
    return dict(ws=WS, wdt=wdt, pcol=pcol, prow=prow, cst=cst)


def core_inputs(inp, shared, c, n_samp=2):
    f = np.float32
    m = dict(shared)
    m["xp"] = _fm(np.asarray(inp["x_prompt"][c], f))
    sidx = [c * n_samp + i for i in range(n_samp)]
    m["xs"] = np.stack([_fm(np.asarray(inp["x_sample"][s], f)) for s in sidx])
    m["memT"] = _fm(np.asarray(inp["mem_prompt"][c], f))
    kvc = np.zeros((n_samp, 2, 128, SLOTW), f)
    sca = np.zeros((n_samp, 2, 128, KC, 30), f)
    scs = np.zeros((n_samp, 2, 128, 24, 3), f)
    ssm = np.zeros((n_samp, 2, 128, 2048), f)
    for i, s in enumerate(sidx):
        for l in range(2):
            kk = np.asarray(inp["cache_mem_k"][l, s], f)
            kvc[i, l, :, 0:2048] = kk.reshape(256, 4, 2, 128).transpose(3, 1, 2, 0).reshape(128, 2048)
            vv = np.asarray(inp["cache_mem_v"][l, s], f).reshape(256, 1024)
            kvc[i, l, :, 2048:4096] = vv.reshape(2, 128, 1024).transpose(1, 0, 2).reshape(128, 2048)
            sca[i, l] = _fm(np.asarray(inp["state_conv_a"][l, s], f))
            scs[i, l] = _fm(np.asarray(inp["state_conv_ssd"][l, s], f))
            ssm[i, l] = np.asarray(inp["state_ssm"][l, s], f).reshape(2048, 128).T
    m.update(kvc=kvc, sca=sca, scs=scs, ssm=ssm)
    return m


def assemble(results, n_cores, T, n_samp=2, t_s=64):
    f = np.float32
    B = n_cores
    yp = np.zeros((B, T, D), f); ys = np.zeros((B * n_samp, t_s, D), f)
    mk = np.zeros((2, B, 256, 4, 256), f); mv = np.zeros((2, B, 256, 4, 256), f)
    cap = np.zeros((2, B, 30, 1024), f); csp = np.zeros((2, B, 3, 3072), f); hp = np.zeros((2, B, 32, 64, 128), f)
    cas = np.zeros((2, B * n_samp, 30, 1024), f); css = np.zeros((2, B * n_samp, 3, 3072), f)
    hs = np.zeros((2, B * n_samp, 32, 64, 128), f)
    for c, r in enumerate(results):
        yp[c] = _fm_inv(r["yp"])
        for l in range(2):
            mk[l, c] = r["mk"][l].reshape(128, 4, 2, 256).transpose(3, 1, 2, 0).reshape(256, 4, 256)
            mv[l, c] = r["mv"][l].reshape(128, 2, 1024).transpose(1, 0, 2).reshape(256, 4, 256)
            cap[l, c] = _fm_inv(r["cap"][l])
            csp[l, c] = _fm_inv(r["csp"][l])
            hp[l, c] = r["hp"][l].T.reshape(32, 64, 128)
        for i in range(n_samp):
            s = c * n_samp + i
            ys[s] = _fm_inv(r["ys"][i])
            for l in range(2):
                cas[l, s] = _fm_inv(r["cas"][i, l])
                css[l, s] = _fm_inv(r["css"][i, l])
                hs[l, s] = r["hs"][i, l].T.reshape(32, 64, 128)
    return (yp, ys, mk, mv, cap, csp, hp, cas, css, hs)


def kernel(**inp):
    n = 8
    T = np.asarray(inp["x_prompt"]).shape[1]
    nc = build(T_PROMPT=T)
    shared = prep_weights(inp)
    in_maps = [core_inputs(inp, shared, c) for c in range(n)]
    res = run_bass_kernel_spmd(nc, in_maps, core_ids=list(range(n)))
    return assemble(res.results, n, T)
```
